# Optimizing a Trainium2 kernel written in Bass

```python
import math
import jax
import jax.numpy as jnp
from jax import lax
import numpy as np

D_MODEL = 2048
BATCH = 2
SEQ = 4096
DEPTH = 2
DEC_BATCH = 128
DEC_SEQ = 1
PAST_LEN = 2048
PAGE_SIZE = 128

N_BRANCH = 3
SC_WIDTH = D_MODEL
SC_CONV = 3
SSM_D_INNER = D_MODEL
SSM_HEAD_DIM = 64
SSM_HEADS = SSM_D_INNER // SSM_HEAD_DIM
SSM_GROUPS = 8
SSM_HPG = SSM_HEADS // SSM_GROUPS
SSM_STATE = 128
SSM_CONV = 4
SSM_CHUNK = 128
SSM_CONV_DIM = SSM_D_INNER + 2 * SSM_GROUPS * SSM_STATE
ATT_HEAD_DIM = 128
ATT_GROUPS = ((128, 1), (512, 4), (2048, 16))
ATT_HPG = 4
ATT_HEADS = ATT_HPG * len(ATT_GROUPS)
ATT_WIDTH = ATT_HEADS * ATT_HEAD_DIM
ATT_OUT_WIDTH = ATT_HPG * ATT_HEAD_DIM
ATT_BLOCK = 128
ATT_SCALE = 1.0 / math.sqrt(ATT_HEAD_DIM)
N_BUCKETS = 32
MAX_DISTANCE = 2048
D_FF = 4 * D_MODEL
EPS = 1e-6
N_IN = N_BRANCH * D_MODEL + 3 * SC_WIDTH + SSM_D_INNER + SSM_CONV_DIM + SSM_HEADS + 3 * ATT_WIDTH

kernel_name = 'hybrid_gated_branch_decoder'


def _in_split_points():
    sizes = [N_BRANCH * D_MODEL, SC_WIDTH, SC_WIDTH, SC_WIDTH, SSM_D_INNER, SSM_CONV_DIM, SSM_HEADS,
             ATT_WIDTH, ATT_WIDTH, ATT_WIDTH]
    pts = []
    acc = 0
    for s in sizes[:-1]:
        acc += s
        pts.append(acc)
    return pts


def rmsnorm(x, g):
    xf = x.astype(jnp.float32)
    xf = xf * lax.rsqrt(jnp.mean(xf * xf, axis=-1, keepdims=True) + EPS)
    return (xf * g.astype(jnp.float32)).astype(x.dtype)


def causal_dwconv(u, buf, w):
    width = w.shape[0]
    L = u.shape[1]
    up = jnp.concatenate([buf.astype(u.dtype), u], axis=1)
    y = up[:, 0:L] * w[0]
    for i in range(1, width):
        y = y + up[:, i:i + L] * w[i]
    return y, up[:, L:]


def t5_bucket(dist):
    d = jnp.asarray(dist, jnp.int32)
    max_exact = N_BUCKETS // 2
    df = jnp.maximum(d, 1).astype(jnp.float32)
    large = max_exact + (jnp.log(df / max_exact) / math.log(MAX_DISTANCE / max_exact)
                         * (N_BUCKETS - max_exact)).astype(jnp.int32)
    large = jnp.minimum(large, N_BUCKETS - 1)
    return jnp.where(d < max_exact, d, large)


def dilated_group_prompt(q, k, v, bias_tab, window, dil):
    B, S, H, hd = q.shape
    nk = window // dil
    L = S // dil
    nb = -(-L // ATT_BLOCK)
    Lp = nb * ATT_BLOCK

    def to_stream(t):
        t = t.reshape(B, L, dil, H, hd).transpose(0, 2, 1, 3, 4)
        t = jnp.pad(t, ((0, 0), (0, 0), (0, Lp - L), (0, 0), (0, 0)))
        return t.reshape(B, dil, nb, ATT_BLOCK, H, hd)

    def with_prev(t):
        prev = jnp.pad(t, ((0, 0), (0, 0), (1, 0), (0, 0), (0, 0), (0, 0)))[:, :, :nb]
        return jnp.concatenate([prev, t], axis=3)

    qs = to_stream(q).astype(jnp.float32)
    kb = with_prev(to_stream(k)).astype(jnp.float32)
    vb = with_prev(to_stream(v)).astype(jnp.float32)
    qi = np.arange(ATT_BLOCK)[:, None]
    kj = np.arange(2 * ATT_BLOCK)[None, :]
    sdist = qi + ATT_BLOCK - kj
    band = (sdist >= 0) & (sdist <= nk)
    valid = band[None] & ((np.arange(nb)[:, None, None] > 0) | (kj[None] >= ATT_BLOCK))
    bias = bias_tab[t5_bucket(np.clip(sdist, 0, nk) * dil)].astype(jnp.float32).transpose(2, 0, 1)
    s = jnp.einsum('brnqhd,brnjhd->brnhqj', qs, kb) * ATT_SCALE + bias
    s = jnp.where(valid[None, None, :, None], s, -jnp.inf)
    m = jnp.max(s, axis=-1)
    p = jnp.exp(s - m[..., None])
    l = jnp.sum(p, axis=-1)
    o = jnp.einsum('brnhqj,brnjhd->brnqhd', p, vb)

    def back_vec(t):
        t = t.transpose(0, 1, 2, 4, 3).reshape(B, dil, Lp, H)[:, :, :L]
        return t.transpose(0, 2, 1, 3).reshape(B, S, H)

    o = o.reshape(B, dil, Lp, H, hd)[:, :, :L].transpose(0, 2, 1, 3, 4).reshape(B, S, H, hd)
    return o, back_vec(m), back_vec(l)


def dilated_group_sample(q, k_new, v_new, buf, bias_tab, window, dil):
    T = q.shape[1]
    Wb = buf.shape[1]
    nk = window // dil
    steps = np.arange(nk + 1)
    idx = Wb + np.arange(T)[:, None] - steps[None, :] * dil
    valid = idx >= 0
    kv_new = jnp.stack([k_new, v_new], axis=2)
    past = buf[:, np.clip(idx, 0, Wb - 1)].astype(kv_new.dtype)
    cur = kv_new[:, np.clip(idx - Wb, 0, T - 1)]
    kvg = jnp.where((idx >= Wb)[None, :, :, None, None, None], cur, past).astype(jnp.float32)
    bias = bias_tab[t5_bucket(steps * dil)].astype(jnp.float32).T
    s = jnp.einsum('bthd,btnhd->bthn', q.astype(jnp.float32), kvg[:, :, :, 0]) * ATT_SCALE + bias
    s = jnp.where(valid[None, :, None, :], s, -jnp.inf)
    m = jnp.max(s, axis=-1)
    p = jnp.exp(s - m[..., None])
    l = jnp.sum(p, axis=-1)
    o = jnp.einsum('bthn,btnhd->bthd', p, kvg[:, :, :, 1])
    return (o, m, l), kv_new


def merge_groups(outs):
    ms = jnp.stack([t[1] for t in outs])
    m_all = jnp.max(ms, axis=0)
    w = jnp.exp(ms - m_all)
    num = w[0][..., None] * outs[0][0]
    den = w[0] * outs[0][2]
    for g in range(1, len(outs)):
        num = num + w[g][..., None] * outs[g][0]
        den = den + w[g] * outs[g][2]
    return num / den[..., None]


def ssd_scan(x, dt, A, Bm, Cm, h0, chunk):
    b, L = x.shape[:2]
    nc = L // chunk

    def c(t):
        return t.reshape((b, nc, chunk) + t.shape[2:])

    x, dt, Bm, Cm = c(x), c(dt), c(Bm), c(Cm)
    acum = jnp.cumsum(dt * A, axis=2)
    diff = acum[:, :, :, None] - acum[:, :, None, :]
    causal = np.tril(np.ones((chunk, chunk), bool))[None, None, :, :, None, None]
    decay = jnp.exp(jnp.where(causal, diff, -jnp.inf))
    cb = jnp.einsum('bcign,bcjgn->bcijg', Cm, Bm)
    wgt = cb[..., None] * decay * dt[:, :, None]
    y_diag = jnp.einsum('bcijge,bcjgep->bcigep', wgt, x)
    decay_end = jnp.exp(acum[:, :, -1:] - acum)
    st = jnp.einsum('bcjgn,bcjge,bcjgep->bcgepn', Bm, decay_end * dt, x)
    chunk_decay = jnp.exp(acum[:, :, -1])

    def step(h, inp):
        s_c, d_c = inp
        return h * d_c[..., None, None] + s_c, h

    hT, h_prev = lax.scan(step, h0, (jnp.moveaxis(st, 1, 0), jnp.moveaxis(chunk_decay, 1, 0)))
    h_prev = jnp.moveaxis(h_prev, 0, 1)
    y_off = jnp.einsum('bcign,bcgepn,bcige->bcigep', Cm, h_prev, jnp.exp(acum))
    y = (y_diag + y_off).reshape((b, L) + x.shape[3:])
    return y, hT


def mamba_mixer(z, xbc, dt_raw, conv_buf, h0, conv_w, conv_b, dt_bias, A_log, D_skip, norm_g):
    b, L = z.shape[:2]
    xbc, new_buf = causal_dwconv(xbc, conv_buf, conv_w)
    xbc = jax.nn.silu(xbc + conv_b)
    xs = xbc[..., :SSM_D_INNER].reshape(b, L, SSM_GROUPS, SSM_HPG, SSM_HEAD_DIM).astype(jnp.float32)
    Bm = xbc[..., SSM_D_INNER:SSM_D_INNER + SSM_GROUPS * SSM_STATE].reshape(b, L, SSM_GROUPS, SSM_STATE).astype(jnp.float32)
    Cm = xbc[..., SSM_D_INNER + SSM_GROUPS * SSM_STATE:].reshape(b, L, SSM_GROUPS, SSM_STATE).astype(jnp.float32)
    dt = jax.nn.softplus(dt_raw.astype(jnp.float32) + dt_bias.astype(jnp.float32)).reshape(b, L, SSM_GROUPS, SSM_HPG)
    A = -jnp.exp(A_log.astype(jnp.float32)).reshape(SSM_GROUPS, SSM_HPG)
    h0 = h0.astype(jnp.float32).reshape(b, SSM_GROUPS, SSM_HPG, SSM_HEAD_DIM, SSM_STATE)
    y, hT = ssd_scan(xs, dt, A, Bm, Cm, h0, math.gcd(L, SSM_CHUNK))
    y = y + D_skip.astype(jnp.float32).reshape(SSM_GROUPS, SSM_HPG)[:, :, None] * xs
    y = y.reshape(b, L, SSM_D_INNER) * jax.nn.silu(z.astype(jnp.float32))
    yg = y.reshape(b, L, SSM_GROUPS, SSM_D_INNER // SSM_GROUPS)
    yg = yg * lax.rsqrt(jnp.mean(yg * yg, axis=-1, keepdims=True) + EPS)
    y = yg.reshape(b, L, SSM_D_INNER) * norm_g.astype(jnp.float32)
    return y.astype(z.dtype), new_buf, hT.reshape(b, SSM_HEADS, SSM_HEAD_DIM, SSM_STATE)


def setup_inputs(seed: int = 0) -> dict:
    key = jax.random.key(seed)
    ks = jax.random.split(key, 32)
    f32 = jnp.float32

    def nrm(k, shape, scale):
        return jax.random.normal(k, shape, f32) * scale

    def gain(k, shape):
        return 1.0 + 0.02 * jax.random.normal(k, shape, f32)

    kv_shape = lambda w: (DEPTH, DEC_BATCH, min(w, PAST_LEN), 2, ATT_HPG, ATT_HEAD_DIM)
    dt = jnp.exp(jax.random.uniform(ks[13], (DEPTH, SSM_HEADS), f32, math.log(1e-3), math.log(1e-1)))
    return {
        'x_prompt': nrm(ks[0], (BATCH, SEQ, D_MODEL), 1.0),
        'x_sample': nrm(ks[1], (DEC_BATCH, DEC_SEQ, D_MODEL), 1.0),
        'state_sc_conv': nrm(ks[2], (DEPTH, DEC_BATCH, SC_CONV - 1, SC_WIDTH), 1.0),
        'state_ssm_conv': nrm(ks[3], (DEPTH, DEC_BATCH, SSM_CONV - 1, SSM_CONV_DIM), 1.0),
        'state_ssm': nrm(ks[4], (DEPTH, DEC_BATCH, SSM_HEADS, SSM_HEAD_DIM, SSM_STATE), 0.1),
        'cache_kv_w128': nrm(ks[5], kv_shape(ATT_GROUPS[0][0]), 1.0),
        'cache_kv_w512': nrm(ks[6], kv_shape(ATT_GROUPS[1][0]), 1.0),
        'cache_kv_w2048': nrm(ks[7], kv_shape(ATT_GROUPS[2][0]), 1.0),
        'norm1_g': gain(ks[8], (DEPTH, D_MODEL)),
        'w_in': nrm(ks[9], (DEPTH, D_MODEL, N_IN), D_MODEL ** -0.5),
        'sc_conv_w': nrm(ks[10], (DEPTH, SC_CONV, SC_WIDTH), SC_CONV ** -0.5),
        'ssm_conv_w': nrm(ks[11], (DEPTH, SSM_CONV, SSM_CONV_DIM), SSM_CONV ** -0.5),
        'ssm_conv_b': nrm(ks[12], (DEPTH, SSM_CONV_DIM), 0.02),
        'ssm_dt_bias': dt + jnp.log(-jnp.expm1(-dt)),
        'ssm_A_log': jnp.log(jax.random.uniform(ks[14], (DEPTH, SSM_HEADS), f32, 1.0, 16.0)),
        'ssm_D': 1.0 + 0.1 * jax.random.normal(ks[15], (DEPTH, SSM_HEADS), f32),
        'ssm_norm_g': gain(ks[16], (DEPTH, SSM_D_INNER)),
        'q_norm_g': gain(ks[17], (DEPTH, ATT_HEAD_DIM)),
        'k_norm_g': gain(ks[18], (DEPTH, ATT_HEAD_DIM)),
        'rel_bias': nrm(ks[19], (N_BUCKETS, ATT_HEADS), 0.5),
        'w_br_sc': nrm(ks[20], (DEPTH, SC_WIDTH, D_MODEL), SC_WIDTH ** -0.5),
        'w_br_ssm': nrm(ks[21], (DEPTH, SSM_D_INNER, D_MODEL), SSM_D_INNER ** -0.5),
        'w_br_att': nrm(ks[22], (DEPTH, ATT_OUT_WIDTH, D_MODEL), ATT_OUT_WIDTH ** -0.5),
        'w_out': nrm(ks[23], (DEPTH, D_MODEL, D_MODEL), D_MODEL ** -0.5),
        'norm2_g': gain(ks[24], (DEPTH, D_MODEL)),
        'w_up': nrm(ks[25], (DEPTH, D_MODEL, D_FF), D_MODEL ** -0.5),
        'w_down': nrm(ks[26], (DEPTH, D_FF, D_MODEL), D_FF ** -0.5),
    }


def reference(x_prompt, x_sample, state_sc_conv, state_ssm_conv, state_ssm, cache_kv_w128, cache_kv_w512,
              cache_kv_w2048, norm1_g, w_in, sc_conv_w, ssm_conv_w, ssm_conv_b, ssm_dt_bias, ssm_A_log, ssm_D,
              ssm_norm_g, q_norm_g, k_norm_g, rel_bias, w_br_sc, w_br_ssm, w_br_att, w_out, norm2_g, w_up, w_down):
    split_pts = _in_split_points()

    def block(x, l, sc_buf, ssm_buf, ssm_h0, kv_bufs):
        b, L, _ = x.shape
        h = rmsnorm(x, norm1_g[l])
        proj = h @ w_in[l]
        gate_r, sc_b, sc_c, sc_x, z, xbc, dt_raw, q, k, v = jnp.split(proj, split_pts, axis=-1)
        gates = jax.nn.sigmoid(gate_r).reshape(b, L, N_BRANCH, D_MODEL)
        conv_u, sc_new = causal_dwconv(sc_c * sc_x, sc_buf, sc_conv_w[l])
        y_sc = sc_b * conv_u
        y_ssm, ssm_conv_new, ssm_hT = mamba_mixer(z, xbc, dt_raw, ssm_buf, ssm_h0, ssm_conv_w[l], ssm_conv_b[l],
                                                  ssm_dt_bias[l], ssm_A_log[l], ssm_D[l], ssm_norm_g[l])
        q = rmsnorm(q.reshape(b, L, ATT_HEADS, ATT_HEAD_DIM), q_norm_g[l])
        k = rmsnorm(k.reshape(b, L, ATT_HEADS, ATT_HEAD_DIM), k_norm_g[l])
        v = v.reshape(b, L, ATT_HEADS, ATT_HEAD_DIM)
        outs = []
        kv_new = []
        for gi, (window, dil) in enumerate(ATT_GROUPS):
            sl = slice(gi * ATT_HPG, (gi + 1) * ATT_HPG)
            if kv_bufs is None:
                outs.append(dilated_group_prompt(q[:, :, sl], k[:, :, sl], v[:, :, sl], rel_bias[:, sl], window, dil))
                keep = min(window, L)
                kv_new.append(jnp.stack([k[:, L - keep:, sl], v[:, L - keep:, sl]], axis=2))
            else:
                res, rows = dilated_group_sample(q[:, :, sl], k[:, :, sl], v[:, :, sl], kv_bufs[gi],
                                                 rel_bias[:, sl], window, dil)
                outs.append(res)
                kv_new.append(rows)
        y_att = merge_groups(outs).reshape(b, L, ATT_OUT_WIDTH).astype(x.dtype)
        merged = (gates[:, :, 0] * (y_sc @ w_br_sc[l]) + gates[:, :, 1] * (y_ssm @ w_br_ssm[l])
                  + gates[:, :, 2] * (y_att @ w_br_att[l]))
        x = x + merged @ w_out[l]
        h2 = rmsnorm(x, norm2_g[l])
        x = x + jnp.square(jax.nn.relu(h2 @ w_up[l])) @ w_down[l]
        return x, (sc_new, ssm_conv_new, ssm_hT, kv_new[0], kv_new[1], kv_new[2])

    xp = x_prompt
    xs = x_sample
    p_new = []
    s_new = []
    for l in range(DEPTH):
        zeros_sc = jnp.zeros((BATCH, SC_CONV - 1, SC_WIDTH), xp.dtype)
        zeros_conv = jnp.zeros((BATCH, SSM_CONV - 1, SSM_CONV_DIM), xp.dtype)
        zeros_h = jnp.zeros((BATCH, SSM_HEADS, SSM_HEAD_DIM, SSM_STATE), jnp.float32)
        xp, st = block(xp, l, zeros_sc, zeros_conv, zeros_h, None)
        p_new.append(st)
        xs, st = block(xs, l, state_sc_conv[l], state_ssm_conv[l], state_ssm[l],
                       (cache_kv_w128[l], cache_kv_w512[l], cache_kv_w2048[l]))
        s_new.append(st)
    p_sc, p_ssm_conv, p_ssm, p_kv128, p_kv512, p_kv2048 = [jnp.stack(a) for a in zip(*p_new)]
    s_sc, s_ssm_conv, s_ssm, s_kv128, s_kv512, s_kv2048 = [jnp.stack(a) for a in zip(*s_new)]
    return (xp, xs, p_sc, p_ssm_conv, p_ssm, p_kv128, p_kv512, p_kv2048,
            s_sc, s_ssm_conv, s_ssm, s_kv128, s_kv512, s_kv2048)
```

```python
import contextlib
import math
import numpy as np
import concourse.bass as bass
import concourse.mybir as mybir
from concourse.bass_utils import run_bass_kernel_spmd

F32 = mybir.dt.float32
BF16 = mybir.dt.bfloat16
AF = mybir.ActivationFunctionType
ALU = mybir.AluOpType
AX = mybir.AxisListType

D = 2048
DEPTH = 2
NS = 16
TB = 512
N_IN = 23072
PROJ_ROWS = 23168
QKV0 = 18464
QKVR = 18560
DT0 = 18432
EPS = 1e-6
SCALE = 1.0 / math.sqrt(128.0)
GROUPS = ((128, 1), (512, 4), (2048, 16))
NEG = -30000.0
NPRM = 280


KSTOP = 0
DBG_KIND = 'Internal'
DBG_MODE = None
DBG_N = 2048
DBG_RANGES = None


class StopBuild(Exception):
    pass


class Sem:
    __slots__ = ("h", "v")


class TT:
    __slots__ = ("w", "r", "ep")

    def __init__(self):
        self.w = {}
        self.r = {}
        self.ep = 0


class Tile:
    __slots__ = ("t", "tr", "sem")


class Sched:
    def __init__(self, nc, es, ndma=64):
        self.nc = nc
        self.es = es
        self.eng = {"pe": nc.tensor, "act": nc.scalar, "dve": nc.vector, "pool": nc.gpsimd, "sp": nc.sync}
        self.esem = {k: self._mk("e_" + k) for k in self.eng}
        self.free = [self._mk("d%d" % i) for i in range(ndma)]
        self.all_d = list(self.free)
        self.known = {k: {} for k in self.eng}
        self.known["pe"][self.esem["pe"]] = 1 << 60
        self.ep = 0
        self.uid = 0
        self.psb = []
        self.psi = 0
        self.nphase = 0

    def _mk(self, name):
        m = Sem()
        m.h = self.es.enter_context(self.nc.semaphore(name))
        m.v = 0
        return m

    def wait(self, e, sem, val):
        if val <= 0:
            return
        kn = self.known[e]
        if kn.get(sem, 0) >= val:
            return
        self.eng[e].wait_ge(sem.h, val)
        kn[sem] = val

    def _fresh(self, t):
        if t.ep != self.ep:
            t.w = {}
            t.r = {}
            t.ep = self.ep

    def deps(self, e, reads, writes):
        for t in reads:
            self._fresh(t)
            for sem, v in t.w.items():
                self.wait(e, sem, v)
        for t in writes:
            self._fresh(t)
            for sem, v in t.w.items():
                self.wait(e, sem, v)
            for sem, v in t.r.items():
                self.wait(e, sem, v)

    def op(self, e, fn, reads=(), writes=()):
        self.deps(e, reads, writes)
        ins = fn(self.eng[e])
        m = self.esem[e]
        m.v += 1
        ins.then_inc(m.h, 1)
        for t in reads:
            t.r[m] = m.v
        for t in writes:
            t.w[m] = m.v

    def dma(self, q, out, in_, sem, reads=(), writes=(), **kw):
        self.deps(q, reads, writes)
        self.wait(q, sem, sem.v)
        ins = self.eng[q].dma_start(out=out, in_=in_, **kw)
        sem.v += 16
        ins.then_inc(sem.h, 16)
        for t in reads:
            t.r[sem] = sem.v
        for t in writes:
            t.w[sem] = sem.v

    def barrier(self):
        e0 = "sp"
        for k, m in self.esem.items():
            if k != e0:
                self.wait(e0, m, m.v)
        for m in self.all_d:
            self.wait(e0, m, m.v)
        ins = self.eng[e0].nop()
        m0 = self.esem[e0]
        m0.v += 1
        ins.then_inc(m0.h, 1)
        for k in self.eng:
            if k != e0:
                self.eng[k].wait_ge(m0.h, m0.v)
            kn = self.known[k]
            for m in self.esem.values():
                if kn.get(m, 0) < m.v:
                    kn[m] = m.v
            for m in self.all_d:
                kn[m] = m.v
        self.ep += 1

    def ps(self):
        p = self.psb[self.psi % len(self.psb)]
        self.psi += 1
        return p

    def mktile(self, es, name, shape, dt, space="sb", sem=None):
        self.uid += 1
        t = Tile()
        nm = "%s_%d" % (name, self.uid)
        if space == "sb":
            t.t = es.enter_context(self.nc.sbuf_tensor(nm, shape, dt))
        else:
            t.t = es.enter_context(self.nc.psum_tensor(nm, shape, dt))
        t.tr = TT()
        t.sem = sem
        return t


class Phase:
    def __init__(self, S):
        self.S = S
        self.es = contextlib.ExitStack()
        self.sems = []
        self.rots = {}

    def sb(self, name, shape, dt, dma=False):
        sem = None
        if dma:
            sem = self.S.free.pop()
            self.sems.append(sem)
        return self.S.mktile(self.es, name, shape, dt, "sb", sem)

    def rot(self, name, n, shape, dt):
        if name not in self.rots:
            self.rots[name] = [[self.sb(name + str(i), shape, dt, dma=True) for i in range(n)], 0]
        r = self.rots[name]
        t = r[0][r[1] % n]
        r[1] += 1
        return t

    def close(self):
        self.S.barrier()
        self.es.close()
        self.S.free.extend(self.sems)
        self.S.nphase += 1
        if KSTOP and self.S.nphase >= KSTOP:
            raise StopBuild()


class DT_:
    def __init__(self, nc, name, shape, dt, kind="Internal"):
        self.ap = nc.dram_tensor(name, shape, dt, kind=kind).ap()
        self.tr = TT()


def build(SEQ):
    NB = SEQ // TB
    nc = bass.Bass("TRN2", target_bir_lowering=False)
    es = contextlib.ExitStack()
    S = Sched(nc, es)
    op = S.op
    dma = S.dma

    def din(name, shape, dt=F32):
        return DT_(nc, name, shape, dt, "ExternalInput")

    def dout(name, shape, dt=F32):
        return DT_(nc, name, shape, dt, "ExternalOutput")

    def dscr(name, shape, dt=F32):
        return DT_(nc, name, shape, dt, "Internal")

    x_p = din("x_p", [SEQ, D])
    x_s = din("x_s", [NS, D])
    st_sc = din("st_sc", [DEPTH, NS * 2, D])
    st_cv = din("st_cv", [DEPTH, NS * 3, 4096])
    st_h = din("st_h", [DEPTH, NS, 2048, 128])
    caches = [din("ca%d" % g, [DEPTH, NS, min(w, 2048), 1024]) for g, (w, dl) in enumerate(GROUPS)]
    w_in = din("w_in", [DEPTH, D, N_IN])
    w_sc = din("w_sc", [DEPTH, D, D])
    w_ssm = din("w_ssm", [DEPTH, D, D])
    w_att = din("w_att", [DEPTH, 512, D])
    w_out = din("w_out", [DEPTH, D, D])
    w_up = din("w_up", [DEPTH, D, 4 * D])
    w_dn = din("w_dn", [DEPTH, 4 * D, D])
    prm = din("prm", [DEPTH, 128, NPRM])
    c_ident = din("c_ident", [128, 128])
    c_ones = din("c_ones", [128, 128])
    c_mask4 = din("c_mask4", [128, 512])
    c_selh = din("c_selh", [32, 32 * 128])
    c_selh2 = din("c_selh2", [32, 16 * 128])
    c_sels = din("c_sels", [16, 16 * 128])
    c_tab = din("c_tab", [128, 24 * 128])
    c_tabs = din("c_tabs", [128, 12])
    c_b0 = din("c_b0", [128, 12 * NS])
    y_p = dout("y_p", [SEQ, D])
    y_s = dout("y_s", [NS, D])
    o_psc = dout("o_psc", [DEPTH, 2, D])
    o_pcv = dout("o_pcv", [DEPTH, 3, 4096])
    o_ph = dout("o_ph", [DEPTH, 2048, 128])
    KEEP = [min(w, SEQ) for w, dl in GROUPS]
    o_pkv = [dout("o_pkv%d" % g, [DEPTH, KEEP[g], 1024]) for g in range(3)]
    o_ssc = dout("o_ssc", [DEPTH, NS * 2, D])
    o_scv = dout("o_scv", [DEPTH, NS * 3, 4096])
    o_sh = dout("o_sh", [DEPTH, NS, 2048, 128])
    o_skv = [dout("o_skv%d" % g, [DEPTH, NS, 1024]) for g in range(3)]
    XT = [dscr("XT%d" % i, [D, SEQ]) for i in range(DEPTH + 1)]
    XS = [dscr("XS%d" % i, [D, NS]) for i in range(DEPTH + 1)]
    X1T = dscr("X1T", [D, TB])
    HT = dscr("HT", [D, TB], BF16)
    PROJ = dscr("PROJ", [PROJ_ROWS, TB])
    XBC = dscr("XBC", [4096, TB])
    DTAG = dscr("DTAG", [64, TB])
    YT = dscr("YT", [D, TB])
    YSC = DT_(nc, "YSC", [D, TB], BF16, DBG_KIND)
    YSSM = DT_(nc, "YSSM", [D, TB], BF16, DBG_KIND)
    YATT = DT_(nc, "YATT", [512, TB], BF16, DBG_KIND)
    MRG = dscr("MRG", [D, TB], BF16)
    AT = dscr("AT", [4 * D, TB], BF16)
    QT = dscr("QT", [1536, TB], BF16)
    QF = dscr("QF", [1536, NS])
    KF = dscr("KF", [1536, NS])
    VF = dscr("VF", [1536, NS])
    QTOK = dscr("QTOK", [NS, 1536])
    KT = [dscr("KT%d" % i, [1536, SEQ], BF16) for i in range(DEPTH)]
    VV = [dscr("VV%d" % i, [SEQ, 1536], BF16) for i in range(DEPTH)]

    def ptile(name, shape, dt=F32):
        t = S.mktile(es, name, shape, dt, "sb", S.free.pop())
        return t

    ident = ptile("ident", [128, 128])
    ones = ptile("ones", [128, 128])
    prm_sb = [ptile("prm%d" % l, [128, NPRM]) for l in range(DEPTH)]
    halo_sc = ptile("halo_sc", [128, 16, 2])
    halo_cv = ptile("halo_cv", [128, 32, 3])
    Hst = ptile("Hst", [128, 2048])
    zero1 = ptile("zero1", [128, 1])
    nega = ptile("nega", [32, 1])
    S.psb = [S.mktile(es, "psb%d" % i, [128, 512], F32, "ps") for i in range(7)]
    pslong = S.mktile(es, "pslong", [128, 512], F32, "ps")

    dma("sp", ident.t[:], c_ident.ap, ident.sem, writes=[ident.tr])
    dma("sp", ones.t[:], c_ones.ap, ones.sem, writes=[ones.tr])
    for l in range(DEPTH):
        dma("sp", prm_sb[l].t[:], prm.ap[l], prm_sb[l].sem, writes=[prm_sb[l].tr])
    op("dve", lambda e: e.memset(zero1.t[:], 0.0), writes=[zero1.tr])
    S.barrier()

    P_N1, P_N2, P_SNG, P_DCOL, P_SCW, P_SSW, P_SSB, P_QG, P_KG, P_DTB, P_ALOG = 0, 16, 32, 48, 64, 112, 240, 272, 273, 274, 275

    def transpose_to(ph, src_ap, m, n, dst_ap, dst_tr, src_tr, eng="dve"):
        ps = S.ps()
        op("pe", lambda e: e.transpose(ps.t[0:n, 0:m], src_ap, ident.t[0:m, 0:m]), reads=[src_tr, ident.tr], writes=[ps.tr])
        if eng == "dve":
            op("dve", lambda e: e.tensor_copy(out=dst_ap, in_=ps.t[0:n, 0:m]), reads=[ps.tr], writes=[dst_tr])
        else:
            op("act", lambda e: e.activation(out=dst_ap, in_=ps.t[0:n, 0:m], func=AF.Copy), reads=[ps.tr], writes=[dst_tr])

    def rstd_from_ps(ph, ps, np_, T, inv_n, tagn):
        r1 = ph.rot("r1" + tagn, 2, [128, T], F32)
        op("dve", lambda e: e.tensor_scalar(out=r1.t[0:np_, :], in0=ps.t[0:np_, 0:T], scalar1=inv_n, scalar2=EPS, op0=ALU.mult, op1=ALU.add), reads=[ps.tr], writes=[r1.tr])
        op("act", lambda e: e.activation(out=r1.t[0:np_, :], in_=r1.t[0:np_, :], func=AF.Sqrt), reads=[r1.tr], writes=[r1.tr])
        r2 = ph.rot("r2" + tagn, 2, [128, T], F32)
        op("dve", lambda e: e.reciprocal(out=r2.t[0:np_, :], in_=r1.t[0:np_, :]), reads=[r1.tr], writes=[r2.tr])
        return r2

    def in_transpose(src, T, dst, c0):
        ph = Phase(S)
        for t0 in range(0, T, 128):
            m = min(128, T - t0)
            xt = ph.rot("xt", 2, [128, D], F32)
            dma("sp", xt.t[0:m, :], src.ap[t0:t0 + m, :], xt.sem, reads=[src.tr], writes=[xt.tr])
            o = ph.rot("o", 2, [128, 16, 128], F32)
            for kc in range(16):
                transpose_to(ph, xt.t[0:m, kc * 128:(kc + 1) * 128], m, 128, o.t[:, kc, 0:m], o.tr, xt.tr, "dve" if kc % 2 else "act")
            dma("sp", dst.ap[:, c0 + t0:c0 + t0 + m].rearrange("(kc p) t -> p kc t", p=128), o.t[:, :, 0:m], o.sem, reads=[o.tr], writes=[dst.tr])
        ph.close()

    def out_transpose(src, c0, T, dst):
        ph = Phase(S)
        for t0 in range(0, T, 128):
            m = min(128, T - t0)
            xt = ph.rot("xt", 2, [128, 16, 128], F32)
            dma("sp", xt.t[:, :, 0:m], src.ap[:, c0 + t0:c0 + t0 + m].rearrange("(kc p) t -> p kc t", p=128), xt.sem, reads=[src.tr], writes=[xt.tr])
            o = ph.rot("o", 2, [128, D], F32)
            for kc in range(16):
                transpose_to(ph, xt.t[:, kc, 0:m], 128, m, o.t[0:m, kc * 128:(kc + 1) * 128], o.tr, xt.tr, "dve" if kc % 2 else "act")
            dma("sp", dst.ap[t0:t0 + m, :], o.t[0:m, :], o.sem, reads=[o.tr], writes=[dst.tr])
        ph.close()

    def norm_phase(src, c0, T, gcol0, l, dst):
        ph = Phase(S)
        x = ph.sb("x", [128, 16, T], F32, dma=True)
        dma("sp", x.t[:], src.ap[:, c0:c0 + T].rearrange("(kc p) t -> p kc t", p=128), x.sem, reads=[src.tr], writes=[x.tr])
        ps = S.ps()
        for kc in range(16):
            sq = ph.rot("sq", 3, [128, T], F32)
            op("act", lambda e: e.activation(out=sq.t[:], in_=x.t[:, kc, :], func=AF.Square), reads=[x.tr], writes=[sq.tr])
            op("pe", lambda e: e.matmul(ps.t[:, 0:T], ones.t[:], sq.t[:], start=(kc == 0), stop=(kc == 15)), reads=[ones.tr, sq.tr], writes=[ps.tr])
        r = rstd_from_ps(ph, ps, 128, T, 1.0 / D, "n")
        hb = ph.sb("hb", [128, 16, T], BF16, dma=True)
        for kc in range(16):
            op("dve", lambda e: e.scalar_tensor_tensor(out=hb.t[:, kc, :], in0=x.t[:, kc, :], scalar=prm_sb[l].t[:, gcol0 + kc:gcol0 + kc + 1], in1=r.t[:], op0=ALU.mult, op1=ALU.mult), reads=[x.tr, r.tr, prm_sb[l].tr], writes=[hb.tr])
        dma("sp", dst.ap[:, 0:T].rearrange("(kc p) t -> p kc t", p=128), hb.t[:], hb.sem, reads=[hb.tr], writes=[dst.tr])
        ph.close()

    def dense(srcs, ranges, T, epi, G=512):
        ph = Phase(S)
        parts = []
        for i, (A, W, KC) in enumerate(srcs):
            for k0 in range(0, KC, 16):
                kn = min(16, KC - k0)
                a = ph.sb("a%d_%d" % (i, k0), [128, kn, T], BF16, dma=True)
                dma("sp", a.t[:], A.ap[k0 * 128:(k0 + kn) * 128, 0:T].rearrange("(kc p) t -> p kc t", p=128), a.sem, reads=[A.tr], writes=[a.tr])
                parts.append((i, k0, kn, a))
        NP_ = len(parts)
        stg = [[ph.sb("wf%d_%d" % (pi, j), [128, parts[pi][2], G], F32, dma=True) for j in range(2)] for pi in range(NP_)]
        wbf = [[ph.sb("wb%d_%d" % (pi, j), [128, parts[pi][2], G], BF16) for j in range(2)] for pi in range(NP_)]
        groups = []
        for (c0, ncol) in ranges:
            for g0 in range(0, ncol, G):
                groups.append((c0 + g0, min(G, ncol - g0)))

        def load(gi):
            col, gsz = groups[gi]
            for pi, (i, k0, kn, a) in enumerate(parts):
                W = srcs[i][1]
                wf = stg[pi][gi % 2]
                dma("act", wf.t[:, :, 0:gsz], W[k0 * 128:(k0 + kn) * 128, col:col + gsz].rearrange("(kc p) n -> p kc n", p=128), wf.sem, writes=[wf.tr])

        def cast(gi):
            col, gsz = groups[gi]
            for pi in range(NP_):
                wf = stg[pi][gi % 2]
                wb = wbf[pi][gi % 2]
                op("pool", lambda e: e.tensor_copy(out=wb.t[:, :, 0:gsz], in_=wf.t[:, :, 0:gsz]), reads=[wf.tr], writes=[wb.tr])

        load(0)
        if len(groups) > 1:
            load(1)
        cast(0)
        for gi, (col, gsz) in enumerate(groups):
            if gi + 1 < len(groups):
                cast(gi + 1)
            if gi + 2 < len(groups):
                load(gi + 2)
            for cc in range(0, gsz, 128):
                csz = min(128, gsz - cc)
                for t0 in range(0, T, 512):
                    tsz = min(512, T - t0)
                    pss = {}
                    for pi, (i, k0, kn, a) in enumerate(parts):
                        if i not in pss:
                            pss[i] = S.ps()
                        ps = pss[i]
                        KC = srcs[i][2]
                        wb = wbf[pi][gi % 2]
                        for kc in range(kn):
                            op("pe", lambda e: e.matmul(ps.t[0:csz, 0:tsz], wb.t[:, kc, cc:cc + csz], a.t[:, kc, t0:t0 + tsz], start=(k0 + kc == 0), stop=(k0 + kc == KC - 1)), reads=[wb.tr, a.tr], writes=[ps.tr])
                    epi(ph, col + cc, csz, t0, tsz, [pss[i] for i in range(len(srcs))])
        ph.close()

    def epi_win(ph, col, csz, t0, tsz, pss):
        row = col if col < QKV0 else col - QKV0 + QKVR
        st = ph.rot("st", 4, [128, 512], F32)
        if col < 6144:
            op("act", lambda e: e.activation(out=st.t[0:csz, 0:tsz], in_=pss[0].t[0:csz, 0:tsz], func=AF.Sigmoid), reads=[pss[0].tr], writes=[st.tr])
        else:
            op("dve", lambda e: e.tensor_copy(out=st.t[0:csz, 0:tsz], in_=pss[0].t[0:csz, 0:tsz]), reads=[pss[0].tr], writes=[st.tr])
        dma("sp", PROJ.ap[row:row + csz, t0:t0 + tsz], st.t[0:csz, 0:tsz], st.sem, reads=[st.tr], writes=[PROJ.tr])

    def epi_merge(ph, col, csz, t0, tsz, pss):
        g = ph.rot("g", 2, [128, 3, 512], F32)
        dma("sp", g.t[:, :, 0:tsz], PROJ.ap[0:6144, t0:t0 + tsz].rearrange("(b f) t -> f b t", b=3)[col:col + 128], g.sem, reads=[PROJ.tr], writes=[g.tr])
        acc = ph.rot("acc", 2, [128, 512], F32)
        tmp = ph.rot("tmp", 2, [128, 512], F32)
        st = ph.rot("st", 3, [128, 512], BF16)
        op("dve", lambda e: e.tensor_tensor(out=acc.t[:, 0:tsz], in0=pss[0].t[:, 0:tsz], in1=g.t[:, 0, 0:tsz], op=ALU.mult), reads=[pss[0].tr, g.tr], writes=[acc.tr])
        op("dve", lambda e: e.tensor_tensor(out=tmp.t[:, 0:tsz], in0=pss[1].t[:, 0:tsz], in1=g.t[:, 1, 0:tsz], op=ALU.mult), reads=[pss[1].tr, g.tr], writes=[tmp.tr])
        op("dve", lambda e: e.tensor_tensor(out=acc.t[:, 0:tsz], in0=acc.t[:, 0:tsz], in1=tmp.t[:, 0:tsz], op=ALU.add), reads=[acc.tr, tmp.tr], writes=[acc.tr])
        op("dve", lambda e: e.tensor_tensor(out=tmp.t[:, 0:tsz], in0=pss[2].t[:, 0:tsz], in1=g.t[:, 2, 0:tsz], op=ALU.mult), reads=[pss[2].tr, g.tr], writes=[tmp.tr])
        op("dve", lambda e: e.tensor_tensor(out=st.t[:, 0:tsz], in0=acc.t[:, 0:tsz], in1=tmp.t[:, 0:tsz], op=ALU.add), reads=[acc.tr, tmp.tr], writes=[st.tr])
        dma("sp", MRG.ap[col:col + 128, t0:t0 + tsz], st.t[:, 0:tsz], st.sem, reads=[st.tr], writes=[MRG.tr])

    def make_epi_res(xsrc, xc0, dst, dc0):
        def epi(ph, col, csz, t0, tsz, pss):
            xi = ph.rot("xi", 3, [128, 512], F32)
            dma("sp", xi.t[:, 0:tsz], xsrc.ap[col:col + 128, xc0 + t0:xc0 + t0 + tsz], xi.sem, reads=[xsrc.tr], writes=[xi.tr])
            st = ph.rot("st", 3, [128, 512], F32)
            op("dve", lambda e: e.tensor_tensor(out=st.t[:, 0:tsz], in0=pss[0].t[:, 0:tsz], in1=xi.t[:, 0:tsz], op=ALU.add), reads=[pss[0].tr, xi.tr], writes=[st.tr])
            dma("sp", dst.ap[col:col + 128, dc0 + t0:dc0 + t0 + tsz], st.t[:, 0:tsz], st.sem, reads=[st.tr], writes=[dst.tr])
        return epi

    def epi_up(ph, col, csz, t0, tsz, pss):
        r = ph.rot("r", 3, [128, 512], F32)
        op("act", lambda e: e.activation(out=r.t[:, 0:tsz], in_=pss[0].t[:, 0:tsz], func=AF.Relu), reads=[pss[0].tr], writes=[r.tr])
        st = ph.rot("st", 3, [128, 512], BF16)
        op("dve", lambda e: e.tensor_tensor(out=st.t[:, 0:tsz], in0=r.t[:, 0:tsz], in1=r.t[:, 0:tsz], op=ALU.mult), reads=[r.tr], writes=[st.tr])
        dma("sp", AT.ap[col:col + 128, t0:t0 + tsz], st.t[:, 0:tsz], st.sem, reads=[st.tr], writes=[AT.tr])

    def conv_phase(l, T, nch, row0, halo, wcol0, ntap, bias_col, store):
        ph = Phase(S)
        nh = ntap - 1
        for kc in range(nch):
            ue = ph.rot("ue", 3, [128, nh + T], F32)
            r = row0 + kc * 128
            if bias_col is None:
                cx = ph.rot("cx", 2, [128, 2, T], F32)
                dma("sp", cx.t[:], PROJ.ap[8192:12288, 0:T].rearrange("(b f) t -> f b t", b=2)[kc * 128:(kc + 1) * 128], cx.sem, reads=[PROJ.tr], writes=[cx.tr])
                op("dve", lambda e: e.tensor_tensor(out=ue.t[:, nh:nh + T], in0=cx.t[:, 0, :], in1=cx.t[:, 1, :], op=ALU.mult), reads=[cx.tr], writes=[ue.tr])
            else:
                dma("sp", ue.t[:, nh:nh + T], PROJ.ap[r:r + 128, 0:T], ue.sem, reads=[PROJ.tr], writes=[ue.tr])
            op("act", lambda e: e.activation(out=ue.t[:, 0:nh], in_=halo.t[:, kc, :], func=AF.Copy), reads=[halo.tr], writes=[ue.tr])
            cv = ph.rot("cv", 3, [128, T], F32)
            wc = prm_sb[l].t
            op("dve", lambda e: e.tensor_scalar(out=cv.t[:], in0=ue.t[:, 0:T], scalar1=wc[:, wcol0 + kc * ntap:wcol0 + kc * ntap + 1], scalar2=None, op0=ALU.mult), reads=[ue.tr, prm_sb[l].tr], writes=[cv.tr])
            for i in range(1, ntap):
                op("dve", lambda e: e.scalar_tensor_tensor(out=cv.t[:], in0=ue.t[:, i:i + T], scalar=wc[:, wcol0 + kc * ntap + i:wcol0 + kc * ntap + i + 1], in1=cv.t[:], op0=ALU.mult, op1=ALU.add), reads=[ue.tr, cv.tr], writes=[cv.tr])
            op("act", lambda e: e.activation(out=halo.t[:, kc, :], in_=ue.t[:, T:T + nh], func=AF.Copy), reads=[ue.tr], writes=[halo.tr])
            store(ph, kc, ue, cv)
        ph.close()

    def sc_store(l, T, dst):
        def store(ph, kc, ue, cv):
            b = ph.rot("b", 2, [128, T], F32)
            dma("sp", b.t[:], PROJ.ap[6144 + kc * 128:6144 + (kc + 1) * 128, 0:T], b.sem, reads=[PROJ.tr], writes=[b.tr])
            st = ph.rot("st", 3, [128, T], BF16)
            op("dve", lambda e: e.tensor_tensor(out=st.t[:], in0=cv.t[:], in1=b.t[:], op=ALU.mult), reads=[cv.tr, b.tr], writes=[st.tr])
            dma("sp", dst.ap[kc * 128:(kc + 1) * 128, 0:T], st.t[:], st.sem, reads=[st.tr], writes=[dst.tr])
        return store

    def cv_store(l, T):
        def store(ph, kc, ue, cv):
            st = ph.rot("st", 3, [128, T], F32)
            op("act", lambda e: e.activation(out=st.t[:], in_=cv.t[:], func=AF.Silu, bias=prm_sb[l].t[:, P_SSB + kc:P_SSB + kc + 1]), reads=[cv.tr, prm_sb[l].tr], writes=[st.tr])
            dma("sp", XBC.ap[kc * 128:(kc + 1) * 128, 0:T], st.t[:], st.sem, reads=[st.tr], writes=[XBC.tr])
        return store

    def halo_out(halo, nch, nh, dst_ap, dst_tr):
        ph = Phase(S)
        o = ph.sb("o", [nh, nch * 128], F32, dma=True)
        for kc in range(nch):
            transpose_to(ph, halo.t[:, kc, :], 128, nh, o.t[0:nh, kc * 128:(kc + 1) * 128], o.tr, halo.tr)
        dma("sp", dst_ap, o.t[:], o.sem, reads=[o.tr], writes=[dst_tr])
        ph.close()

    def halo_in(halo, nch, nh, src_ap, src_tr, ns):
        ph = Phase(S)
        rows = ns * nh
        x = ph.sb("x", [rows, nch * 128], F32, dma=True)
        dma("sp", x.t[:], src_ap, x.sem, reads=[src_tr], writes=[x.tr])
        for kc in range(nch):
            transpose_to(ph, x.t[0:rows, kc * 128:(kc + 1) * 128], rows, 128, halo.t[:, kc, :], halo.tr, x.tr)
        ph.close()

    def dt_phase(l, T, scan):
        ph = Phase(S)
        d = ph.sb("d", [32, T], F32, dma=True)
        dma("sp", d.t[:], PROJ.ap[DT0:DT0 + 32, 0:T], d.sem, reads=[PROJ.tr], writes=[d.tr])
        pr = prm_sb[l]
        op("act", lambda e: e.activation(out=d.t[:], in_=d.t[:], func=AF.Exp, bias=pr.t[0:32, P_DTB:P_DTB + 1]), reads=[d.tr, pr.tr], writes=[d.tr])
        op("act", lambda e: e.activation(out=d.t[:], in_=d.t[:], func=AF.Ln, bias=1.0), reads=[d.tr], writes=[d.tr])
        op("act", lambda e: e.activation(out=nega.t[:], in_=pr.t[0:32, P_ALOG:P_ALOG + 1], func=AF.Exp), reads=[pr.tr], writes=[nega.tr])
        op("dve", lambda e: e.tensor_scalar(out=nega.t[:], in0=nega.t[:], scalar1=-1.0, scalar2=None, op0=ALU.mult), reads=[nega.tr], writes=[nega.tr])
        a = ph.sb("a", [32, T], F32, dma=True)
        op("dve", lambda e: e.tensor_scalar(out=a.t[:], in0=d.t[:], scalar1=nega.t[:, 0:1], scalar2=None, op0=ALU.mult), reads=[d.tr, nega.tr], writes=[a.tr])
        if scan:
            on = ph.sb("on", [32, T], F32)
            op("dve", lambda e: e.memset(on.t[:], 1.0), writes=[on.tr])
            ag = ph.sb("ag", [32, T], F32, dma=True)
            op("dve", lambda e: e.tensor_tensor_scan(out=ag.t[:], data0=on.t[:], data1=a.t[:], initial=0.0, op0=ALU.mult, op1=ALU.add), reads=[on.tr, a.tr], writes=[ag.tr])
            a = ag
        dma("sp", DTAG.ap[0:32, 0:T], d.t[:], d.sem, reads=[d.tr], writes=[DTAG.tr])
        dma("sp", DTAG.ap[32:64, 0:T], a.t[:], a.sem, reads=[a.tr], writes=[DTAG.tr])
        ph.close()

    def ssd_phase(l, T):
        ph = Phase(S)
        mask4 = ph.sb("mask4", [128, 512], F32, dma=True)
        dma("sp", mask4.t[:], c_mask4.ap, mask4.sem, writes=[mask4.tr])
        bc = ph.sb("bc", [128, 16, T], BF16)
        for hf in range(2):
            bcf = ph.rot("bcf", 2, [128, 8, T], F32)
            dma("sp", bcf.t[:], XBC.ap[2048 + hf * 1024:3072 + hf * 1024, 0:T].rearrange("(kc p) t -> p kc t", p=128), bcf.sem, reads=[XBC.tr], writes=[bcf.tr])
            op("pool", lambda e: e.tensor_copy(out=bc.t[:, hf * 8:(hf + 1) * 8, :], in_=bcf.t[:]), reads=[bcf.tr], writes=[bc.tr])
        dtag = ph.sb("dtag", [32, 2, T], F32, dma=True)
        dma("sp", dtag.t[:], DTAG.ap[0:64, 0:T].rearrange("(a h) t -> h a t", a=2), dtag.sem, reads=[DTAG.tr], writes=[dtag.tr])
        Hb = ph.sb("Hb", [128, 2048], BF16)
        for c in range(T // 128):
            cs = c * 128
            xs = ph.rot("xs", 2, [128, 16, 128], F32)
            dma("sp", xs.t[:], XBC.ap[0:2048, cs:cs + 128].rearrange("(kc p) t -> p kc t", p=128), xs.sem, reads=[XBC.tr], writes=[xs.tr])
            bf = ph.rot("bf", 2, [128, 8, 128], F32)
            dma("sp", bf.t[:], XBC.ap[2048:3072, cs:cs + 128].rearrange("(kc p) t -> p kc t", p=128), bf.sem, reads=[XBC.tr], writes=[bf.tr])
            yt = ph.rot("yt", 2, [128, 16, 128], F32)
            nag0 = ph.rot("nag0", 2, [32, 1], F32)
            if c == 0:
                op("dve", lambda e: e.tensor_copy(out=nag0.t[:], in_=zero1.t[0:32, :]), reads=[zero1.tr], writes=[nag0.tr])
            else:
                op("dve", lambda e: e.tensor_scalar(out=nag0.t[:], in0=dtag.t[:, 1, cs - 1:cs], scalar1=-1.0, scalar2=None, op0=ALU.mult), reads=[dtag.tr], writes=[nag0.tr])
            sm = ph.rot("sm", 2, [32, 4, 128], F32)
            op("dve", lambda e: e.tensor_copy(out=sm.t[:, 0, :], in_=dtag.t[:, 0, cs:cs + 128]), reads=[dtag.tr], writes=[sm.tr])
            op("dve", lambda e: e.tensor_scalar(out=sm.t[:, 1, :], in0=dtag.t[:, 1, cs:cs + 128], scalar1=-1.0, scalar2=None, op0=ALU.mult), reads=[dtag.tr], writes=[sm.tr])
            op("act", lambda e: e.activation(out=sm.t[:, 2, :], in_=dtag.t[:, 1, cs:cs + 128], func=AF.Exp, bias=nag0.t[:, 0:1]), reads=[dtag.tr, nag0.tr], writes=[sm.tr])
            op("act", lambda e: e.activation(out=sm.t[:, 3, :], in_=dtag.t[:, 1, cs:cs + 128], func=AF.Exp, scale=-1.0, bias=dtag.t[:, 1, cs + 127:cs + 128]), reads=[dtag.tr], writes=[sm.tr])
            op("dve", lambda e: e.tensor_tensor(out=sm.t[:, 3, :], in0=sm.t[:, 3, :], in1=dtag.t[:, 0, cs:cs + 128], op=ALU.mult), reads=[sm.tr, dtag.tr], writes=[sm.tr])
            pst = S.ps()
            for q in range(4):
                op("pe", lambda e: e.transpose(pst.t[:, q * 32:(q + 1) * 32], sm.t[:, q, :], ident.t[0:32, 0:32]), reads=[sm.tr, ident.tr], writes=[pst.tr])
            tk = ph.rot("tk", 2, [128, 4, 32], F32)
            op("dve", lambda e: e.tensor_copy(out=tk.t[:].rearrange("p a h -> p (a h)"), in_=pst.t[:, 0:128]), reads=[pst.tr], writes=[tk.tr])
            dg = ph.rot("dg", 2, [32, 32], F32)
            op("dve", lambda e: e.tensor_scalar(out=dg.t[:], in0=ident.t[0:32, 0:32], scalar1=sm.t[:, 2, 127:128], scalar2=None, op0=ALU.mult), reads=[ident.tr, sm.tr], writes=[dg.tr])
            pcd = S.ps()
            op("pe", lambda e: e.matmul(pcd.t[:, 0:32], ones.t[0:32, :], dg.t[:], start=True, stop=True), reads=[ones.tr, dg.tr], writes=[pcd.tr])
            cdb = ph.rot("cdb", 2, [128, 32], F32)
            op("act", lambda e: e.activation(out=cdb.t[:], in_=pcd.t[:, 0:32], func=AF.Copy), reads=[pcd.tr], writes=[cdb.tr])
            xtok = ph.rot("xtok", 1, [128, 2048], F32)
            for g4 in range(4):
                p4 = S.ps()
                for q in range(4):
                    kc = g4 * 4 + q
                    op("pe", lambda e: e.transpose(p4.t[:, q * 128:(q + 1) * 128], xs.t[:, kc, :], ident.t[:]), reads=[xs.tr, ident.tr], writes=[p4.tr])
                if g4 % 2:
                    op("act", lambda e: e.activation(out=xtok.t[:, g4 * 512:(g4 + 1) * 512], in_=p4.t[:], func=AF.Copy), reads=[p4.tr], writes=[xtok.tr])
                else:
                    op("dve", lambda e: e.tensor_copy(out=xtok.t[:, g4 * 512:(g4 + 1) * 512], in_=p4.t[:]), reads=[p4.tr], writes=[xtok.tr])
            btok = ph.rot("btok", 2, [128, 1024], BF16)
            for g4 in range(2):
                p4 = S.ps()
                for q in range(4):
                    kc = g4 * 4 + q
                    op("pe", lambda e: e.transpose(p4.t[:, q * 128:(q + 1) * 128], bf.t[:, kc, :], ident.t[:]), reads=[bf.tr, ident.tr], writes=[p4.tr])
                op("act", lambda e: e.activation(out=btok.t[:, g4 * 512:(g4 + 1) * 512], in_=p4.t[:], func=AF.Copy), reads=[p4.tr], writes=[btok.tr])
            xdt = ph.rot("xdt", 2, [128, 2048], BF16)
            op("dve", lambda e: e.tensor_tensor(out=xdt.t[:].rearrange("p (h d) -> p h d", h=32), in0=xtok.t[:].rearrange("p (h d) -> p h d", h=32), in1=tk.t[:, 0, :].unsqueeze(2).to_broadcast([128, 32, 64]), op=ALU.mult), reads=[xtok.tr, tk.tr], writes=[xdt.tr])
            xw = ph.rot("xw", 2, [128, 2048], BF16)
            op("dve", lambda e: e.tensor_tensor(out=xw.t[:].rearrange("p (h d) -> p h d", h=32), in0=xtok.t[:].rearrange("p (h d) -> p h d", h=32), in1=tk.t[:, 3, :].unsqueeze(2).to_broadcast([128, 32, 64]), op=ALU.mult), reads=[xtok.tr, tk.tr], writes=[xw.tr])
            op("dve", lambda e: e.tensor_copy(out=Hb.t[:], in_=Hst.t[:]), reads=[Hst.tr], writes=[Hb.tr])
            yo = ph.rot("yo", 1, [128, 2048], F32)
            for g2 in range(4):
                pz = S.ps()
                for q in range(2):
                    g = g2 * 2 + q
                    op("pe", lambda e: e.matmul(pz.t[:, q * 256:(q + 1) * 256], bc.t[:, 8 + g, cs:cs + 128], Hb.t[:, g * 256:(g + 1) * 256], start=True, stop=True), reads=[bc.tr, Hb.tr], writes=[pz.tr])
                op("dve", lambda e: e.tensor_tensor(out=yo.t[:, g2 * 512:(g2 + 1) * 512].rearrange("p (h d) -> p h d", h=8), in0=pz.t[:].rearrange("p (h d) -> p h d", h=8), in1=tk.t[:, 2, g2 * 8:(g2 + 1) * 8].unsqueeze(2).to_broadcast([128, 8, 64]), op=ALU.mult), reads=[pz.tr, tk.tr], writes=[yo.tr])
            ysb = ph.rot("ysb", 1, [128, 2048], F32)
            for g in range(8):
                pcb = S.ps()
                op("pe", lambda e: e.matmul(pcb.t[:, 0:128], bc.t[:, g, cs:cs + 128], bc.t[:, 8 + g, cs:cs + 128], start=True, stop=True), reads=[bc.tr], writes=[pcb.tr])
                cb = ph.rot("cb", 2, [128, 128], BF16)
                op("act", lambda e: e.activation(out=cb.t[:], in_=pcb.t[:, 0:128], func=AF.Copy), reads=[pcb.tr], writes=[cb.tr])
                pd = S.ps()
                op("pe", lambda e: e.matmul(pd.t[:], ident.t[:], mask4.t[:], start=True, stop=False), reads=[ident.tr, mask4.tr], writes=[pd.tr])
                rr = ph.rot("rr", 2, [32, 4, 128], F32)
                op("dve", lambda e: e.tensor_tensor(out=rr.t[:], in0=dtag.t[:, 1, cs:cs + 128].unsqueeze(1).to_broadcast([32, 4, 128]), in1=ident.t[0:32, g * 4:(g + 1) * 4].unsqueeze(2).to_broadcast([32, 4, 128]), op=ALU.mult), reads=[dtag.tr, ident.tr], writes=[rr.tr])
                op("pe", lambda e: e.matmul(pd.t[:], ones.t[0:32, :], rr.t[:].rearrange("k q i -> k (q i)"), start=False, stop=True), reads=[ones.tr, rr.tr], writes=[pd.tr])
                dec = ph.rot("dec", 2, [128, 512], BF16)
                for q in range(4):
                    h = g * 4 + q
                    op("act", lambda e: e.activation(out=dec.t[:, q * 128:(q + 1) * 128], in_=pd.t[:, q * 128:(q + 1) * 128], func=AF.Exp, bias=tk.t[:, 1, h:h + 1]), reads=[pd.tr, tk.tr], writes=[dec.tr])
                wg = ph.rot("wg", 2, [128, 512], BF16)
                op("dve", lambda e: e.tensor_tensor(out=wg.t[:].rearrange("p (q i) -> p q i", q=4), in0=dec.t[:].rearrange("p (q i) -> p q i", q=4), in1=cb.t[:].unsqueeze(1).to_broadcast([128, 4, 128]), op=ALU.mult), reads=[dec.tr, cb.tr], writes=[wg.tr])
                py = S.ps()
                for q in range(4):
                    h = g * 4 + q
                    op("pe", lambda e: e.matmul(py.t[:, q * 64:(q + 1) * 64], wg.t[:, q * 128:(q + 1) * 128], xdt.t[:, h * 64:(h + 1) * 64], start=True, stop=True), reads=[wg.tr, xdt.tr], writes=[py.tr])
                op("dve", lambda e: e.tensor_tensor(out=ysb.t[:, g * 256:(g + 1) * 256], in0=py.t[:, 0:256], in1=yo.t[:, g * 256:(g + 1) * 256], op=ALU.add), reads=[py.tr, yo.tr], writes=[ysb.tr])
            for g4 in range(4):
                p4 = S.ps()
                for q in range(4):
                    kc = g4 * 4 + q
                    op("pe", lambda e: e.transpose(p4.t[:, q * 128:(q + 1) * 128], ysb.t[:, kc * 128:(kc + 1) * 128], ident.t[:]), reads=[ysb.tr, ident.tr], writes=[p4.tr])
                op("act", lambda e: e.activation(out=yt.t[:, g4 * 4:(g4 + 1) * 4, :], in_=p4.t[:].rearrange("p (q i) -> p q i", q=4), func=AF.Copy), reads=[p4.tr], writes=[yt.tr])
            dma("sp", YT.ap[:, cs:cs + 128].rearrange("(kc p) t -> p kc t", p=128), yt.t[:], yt.sem, reads=[yt.tr], writes=[YT.tr])
            for g2 in range(4):
                pz = S.ps()
                for q in range(2):
                    g = g2 * 2 + q
                    op("pe", lambda e: e.matmul(pz.t[:, q * 256:(q + 1) * 256], btok.t[:, g * 128:(g + 1) * 128], xw.t[:, g * 256:(g + 1) * 256], start=True, stop=True), reads=[btok.tr, xw.tr], writes=[pz.tr])
                op("dve", lambda e: e.tensor_tensor(out=Hst.t[:, g2 * 512:(g2 + 1) * 512].rearrange("p (h d) -> p h d", h=8), in0=Hst.t[:, g2 * 512:(g2 + 1) * 512].rearrange("p (h d) -> p h d", h=8), in1=cdb.t[:, g2 * 8:(g2 + 1) * 8].unsqueeze(2).to_broadcast([128, 8, 64]), op=ALU.mult), reads=[Hst.tr, cdb.tr], writes=[Hst.tr])
                op("dve", lambda e: e.tensor_tensor(out=Hst.t[:, g2 * 512:(g2 + 1) * 512], in0=Hst.t[:, g2 * 512:(g2 + 1) * 512], in1=pz.t[:], op=ALU.add), reads=[Hst.tr, pz.tr], writes=[Hst.tr])
        ph.close()

    def hstate_out(dst_ap, dst_tr):
        ph = Phase(S)
        for m in range(16):
            o = ph.rot("o", 3, [128, 128], F32)
            transpose_to(ph, Hst.t[:, m * 128:(m + 1) * 128], 128, 128, o.t[:], o.tr, Hst.tr)
            dma("sp", dst_ap[m * 128:(m + 1) * 128, :], o.t[:], o.sem, reads=[o.tr], writes=[dst_tr])
        ph.close()

    def ssm_post(l, T, dst):
        ph = Phase(S)
        pr = prm_sb[l]
        for gp in range(8):
            ts_ = []
            ps = S.ps()
            for q in range(2):
                kc = gp * 2 + q
                y = ph.rot("y", 4, [128, T], F32)
                dma("sp", y.t[:], YT.ap[kc * 128:(kc + 1) * 128, 0:T], y.sem, reads=[YT.tr], writes=[y.tr])
                x = ph.rot("x", 4, [128, T], F32)
                dma("sp", x.t[:], XBC.ap[kc * 128:(kc + 1) * 128, 0:T], x.sem, reads=[XBC.tr], writes=[x.tr])
                z = ph.rot("z", 4, [128, T], F32)
                dma("sp", z.t[:], PROJ.ap[12288 + kc * 128:12288 + (kc + 1) * 128, 0:T], z.sem, reads=[PROJ.tr], writes=[z.tr])
                op("dve", lambda e: e.scalar_tensor_tensor(out=y.t[:], in0=x.t[:], scalar=pr.t[:, P_DCOL + kc:P_DCOL + kc + 1], in1=y.t[:], op0=ALU.mult, op1=ALU.add), reads=[x.tr, y.tr, pr.tr], writes=[y.tr])
                op("act", lambda e: e.activation(out=z.t[:], in_=z.t[:], func=AF.Silu), reads=[z.tr], writes=[z.tr])
                op("dve", lambda e: e.tensor_tensor(out=y.t[:], in0=y.t[:], in1=z.t[:], op=ALU.mult), reads=[y.tr, z.tr], writes=[y.tr])
                op("act", lambda e: e.activation(out=x.t[:], in_=y.t[:], func=AF.Square), reads=[y.tr], writes=[x.tr])
                op("pe", lambda e: e.matmul(ps.t[:, 0:T], ones.t[:], x.t[:], start=(q == 0), stop=(q == 1)), reads=[ones.tr, x.tr], writes=[ps.tr])
                ts_.append((kc, y))
            r = rstd_from_ps(ph, ps, 128, T, 1.0 / 256.0, "g")
            for kc, y in ts_:
                st = ph.rot("st", 3, [128, T], BF16)
                op("dve", lambda e: e.scalar_tensor_tensor(out=st.t[:], in0=y.t[:], scalar=pr.t[:, P_SNG + kc:P_SNG + kc + 1], in1=r.t[:], op0=ALU.mult, op1=ALU.mult), reads=[y.tr, r.tr, pr.tr], writes=[st.tr])
                dma("sp", dst.ap[kc * 128:(kc + 1) * 128, 0:T], st.t[:], st.sem, reads=[st.tr], writes=[dst.tr])
        ph.close()

    def att_prep(l, T, tok0, sample):
        ph = Phase(S)
        pr = prm_sb[l]
        for which in range(2):
            for hd in range(12):
                r0 = QKVR + which * 1536 + hd * 128
                x = ph.rot("x", 3, [128, T], F32)
                dma("sp", x.t[:], PROJ.ap[r0:r0 + 128, 0:T], x.sem, reads=[PROJ.tr], writes=[x.tr])
                sq = ph.rot("sq", 2, [128, T], F32)
                op("act", lambda e: e.activation(out=sq.t[:], in_=x.t[:], func=AF.Square), reads=[x.tr], writes=[sq.tr])
                ps = S.ps()
                op("pe", lambda e: e.matmul(ps.t[:, 0:T], ones.t[:], sq.t[:], start=True, stop=True), reads=[ones.tr, sq.tr], writes=[ps.tr])
                r = rstd_from_ps(ph, ps, 128, T, 1.0 / 128.0, "a")
                gc = P_QG + which
                xn = ph.rot("xn", 3, [128, T], F32)
                op("dve", lambda e: e.scalar_tensor_tensor(out=xn.t[:], in0=x.t[:], scalar=pr.t[:, gc:gc + 1], in1=r.t[:], op0=ALU.mult, op1=ALU.mult), reads=[x.tr, r.tr, pr.tr], writes=[xn.tr])
                if sample:
                    dst = QF if which == 0 else KF
                    dma("sp", dst.ap[hd * 128:(hd + 1) * 128, 0:T], xn.t[:], xn.sem, reads=[xn.tr], writes=[dst.tr])
                else:
                    xb = ph.rot("xb", 3, [128, T], BF16)
                    op("act", lambda e: e.activation(out=xb.t[:], in_=xn.t[:], func=AF.Copy), reads=[xn.tr], writes=[xb.tr])
                    if which == 0:
                        dma("sp", QT.ap[hd * 128:(hd + 1) * 128, 0:T], xb.t[:], xb.sem, reads=[xb.tr], writes=[QT.tr])
                    else:
                        dma("sp", KT[l].ap[hd * 128:(hd + 1) * 128, tok0:tok0 + T], xb.t[:], xb.sem, reads=[xb.tr], writes=[KT[l].tr])
                if sample:
                    o = ph.rot("otk", 3, [128, 128], F32)
                    transpose_to(ph, xn.t[:, 0:T], 128, T, o.t[0:T, :], o.tr, xn.tr)
                    if which == 0:
                        dma("sp", QTOK.ap[:, hd * 128:(hd + 1) * 128], o.t[0:T, :], o.sem, reads=[o.tr], writes=[QTOK.tr])
                    else:
                        g, j = hd // 4, hd % 4
                        dma("sp", o_skv[g].ap[l, :, j * 128:(j + 1) * 128], o.t[0:T, :], o.sem, reads=[o.tr], writes=[o_skv[g].tr])
                elif which == 1:
                    g, j = hd // 4, hd % 4
                    first = SEQ - KEEP[g]
                    for t0 in range(0, T, 128):
                        if tok0 + t0 >= first:
                            o = ph.rot("otk", 3, [128, 128], F32)
                            transpose_to(ph, xn.t[:, t0:t0 + 128], 128, 128, o.t[:], o.tr, xn.tr)
                            rr = tok0 + t0 - first
                            dma("sp", o_pkv[g].ap[l, rr:rr + 128, j * 128:(j + 1) * 128], o.t[:], o.sem, reads=[o.tr], writes=[o_pkv[g].tr])
        for hd in range(12):
            g, j = hd // 4, hd % 4
            r0 = QKVR + 2 * 1536 + hd * 128
            x = ph.rot("x", 3, [128, T], F32)
            dma("sp", x.t[:], PROJ.ap[r0:r0 + 128, 0:T], x.sem, reads=[PROJ.tr], writes=[x.tr])
            if sample:
                dma("sp", VF.ap[hd * 128:(hd + 1) * 128, 0:T], x.t[:], x.sem, reads=[x.tr], writes=[VF.tr])
                o = ph.rot("otk", 3, [128, 128], F32)
                transpose_to(ph, x.t[:, 0:T], 128, T, o.t[0:T, :], o.tr, x.tr)
                dma("sp", o_skv[g].ap[l, :, 512 + j * 128:512 + (j + 1) * 128], o.t[0:T, :], o.sem, reads=[o.tr], writes=[o_skv[g].tr])
            else:
                first = SEQ - KEEP[g]
                for t0 in range(0, T, 128):
                    o = ph.rot("otk", 3, [128, 128], F32)
                    transpose_to(ph, x.t[:, t0:t0 + 128], 128, 128, o.t[:], o.tr, x.tr)
                    ob = ph.rot("ob", 3, [128, 128], BF16)
                    op("act", lambda e: e.activation(out=ob.t[:], in_=o.t[:], func=AF.Copy), reads=[o.tr], writes=[ob.tr])
                    dma("sp", VV[l].ap[tok0 + t0:tok0 + t0 + 128, hd * 128:(hd + 1) * 128], ob.t[:], ob.sem, reads=[ob.tr], writes=[VV[l].tr])
                    if tok0 + t0 >= first:
                        rr = tok0 + t0 - first
                        dma("sp", o_pkv[g].ap[l, rr:rr + 128, 512 + j * 128:512 + (j + 1) * 128], o.t[:], o.sem, reads=[o.tr], writes=[o_pkv[g].tr])
        ph.close()

    def att_phase(l, T, tok0):
        ph = Phase(S)
        tab = ph.sb("tab", [128, 24, 128], F32, dma=True)
        dma("sp", tab.t[:], c_tab.ap.rearrange("p (a i) -> p a i", a=24), tab.sem, writes=[tab.tr])
        q = ph.sb("q", [128, 12, T], BF16, dma=True)
        dma("sp", q.t[:], QT.ap[:, 0:T].rearrange("(h p) t -> p h t", p=128), q.sem, reads=[QT.tr], writes=[q.tr])
        onb = ph.sb("onb", [128, 128], BF16)
        op("dve", lambda e: e.tensor_copy(out=onb.t[:], in_=ones.t[:]), reads=[ones.tr], writes=[onb.tr])
        num = ph.sb("num", [128, 4, T], F32)
        den = ph.sb("den", [128, 4, T], F32)
        first = True
        for g, (win, dil) in enumerate(GROUPS):
            span = 128 * dil
            blk0 = tok0 // span
            w0 = max(0, (blk0 - 1) * span)
            w1 = tok0 + T
            wl = w1 - w0
            kw = ph.sb("kw%d" % g, [128, 4, wl], BF16, dma=True)
            dma("sp", kw.t[:], KT[l].ap[g * 512:(g + 1) * 512, w0:w1].rearrange("(h p) t -> p h t", p=128), kw.sem, reads=[KT[l].tr], writes=[kw.tr])
            for r in range(dil):
                nq_tot = T // dil
                for q0 in range(0, nq_tot, 128):
                    nq = min(128, nq_tot - q0)
                    sq0 = (tok0 // dil) + q0
                    n = sq0 // 128
                    i0 = sq0 % 128
                    qcol0 = r + dil * q0
                    for j in range(4):
                        hd = g * 4 + j
                        pn = S.ps()
                        pdn = S.ps()
                        kts = []
                        if n >= 1:
                            kts.append((0, (n - 1) * 128, 128))
                        kts.append((1, n * 128, i0 + nq))
                        for ki, (half, sk0, nk) in enumerate(kts):
                            tk0 = r + dil * sk0
                            kc0 = tk0 - w0
                            pst_ = S.ps()
                            op("pe", lambda e: e.matmul(pst_.t[0:nk, 0:nq], kw.t[:, j, kc0:kc0 + dil * (nk - 1) + 1:dil], q.t[:, hd, qcol0:qcol0 + dil * (nq - 1) + 1:dil], start=True, stop=True), reads=[kw.tr, q.tr], writes=[pst_.tr])
                            sc = ph.rot("sc", 3, [128, 128], F32)
                            op("dve", lambda e: e.scalar_tensor_tensor(out=sc.t[0:nk, 0:nq], in0=pst_.t[0:nk, 0:nq], scalar=SCALE, in1=tab.t[0:nk, (g * 4 + j) * 2 + half, i0:i0 + nq], op0=ALU.mult, op1=ALU.add), reads=[pst_.tr, tab.tr], writes=[sc.tr])
                            pt = ph.rot("pt", 3, [128, 128], BF16)
                            op("act", lambda e: e.activation(out=pt.t[0:nk, 0:nq], in_=sc.t[0:nk, 0:nq], func=AF.Exp), reads=[sc.tr], writes=[pt.tr])
                            vt = ph.rot("vt", 4, [128, 128], BF16)
                            dma("sp", vt.t[0:nk, :], VV[l].ap[tk0:tk0 + dil * (nk - 1) + 1:dil, hd * 128:(hd + 1) * 128], vt.sem, reads=[VV[l].tr], writes=[vt.tr])
                            op("pe", lambda e: e.matmul(pn.t[:, 0:nq], vt.t[0:nk, :], pt.t[0:nk, 0:nq], start=(ki == 0), stop=(ki == len(kts) - 1)), reads=[vt.tr, pt.tr], writes=[pn.tr])
                            op("pe", lambda e: e.matmul(pdn.t[:, 0:nq], onb.t[0:nk, :], pt.t[0:nk, 0:nq], start=(ki == 0), stop=(ki == len(kts) - 1)), reads=[onb.tr, pt.tr], writes=[pdn.tr])
                        ncols = slice(qcol0, qcol0 + dil * (nq - 1) + 1, dil)
                        if first:
                            op("act", lambda e: e.activation(out=num.t[:, j, ncols], in_=pn.t[:, 0:nq], func=AF.Copy), reads=[pn.tr], writes=[num.tr])
                            op("act", lambda e: e.activation(out=den.t[:, j, ncols], in_=pdn.t[:, 0:nq], func=AF.Copy), reads=[pdn.tr], writes=[den.tr])
                        else:
                            op("dve", lambda e: e.tensor_tensor(out=num.t[:, j, ncols], in0=num.t[:, j, ncols], in1=pn.t[:, 0:nq], op=ALU.add), reads=[pn.tr, num.tr], writes=[num.tr])
                            op("dve", lambda e: e.tensor_tensor(out=den.t[:, j, ncols], in0=den.t[:, j, ncols], in1=pdn.t[:, 0:nq], op=ALU.add), reads=[pdn.tr, den.tr], writes=[den.tr])
            first = False
        op("dve", lambda e: e.reciprocal(out=den.t[:], in_=den.t[:]), reads=[den.tr], writes=[den.tr])
        yb = ph.sb("yb", [128, 4, T], BF16, dma=True)
        op("dve", lambda e: e.tensor_tensor(out=yb.t[:], in0=num.t[:], in1=den.t[:], op=ALU.mult), reads=[num.tr, den.tr], writes=[yb.tr])
        dma("sp", YATT.ap[:, 0:T].rearrange("(h p) t -> p h t", p=128), yb.t[:], yb.sem, reads=[yb.tr], writes=[YATT.tr])
        ph.close()

    def ssd_sample(l):
        ph = Phase(S)
        T = NS
        pr = prm_sb[l]
        selh2 = ph.sb("selh2", [32, 16, 128], F32, dma=True)
        dma("sp", selh2.t[:], c_selh2.ap.rearrange("k (m p) -> k m p", m=16), selh2.sem, writes=[selh2.tr])
        xs = ph.sb("xs", [128, 16, T], F32, dma=True)
        dma("sp", xs.t[:], XBC.ap[0:2048, 0:T].rearrange("(kc p) t -> p kc t", p=128), xs.sem, reads=[XBC.tr], writes=[xs.tr])
        bcs = ph.sb("bcs", [128, 16, T], F32, dma=True)
        dma("sp", bcs.t[:], XBC.ap[2048:4096, 0:T].rearrange("(kc p) t -> p kc t", p=128), bcs.sem, reads=[XBC.tr], writes=[bcs.tr])
        dtag = ph.sb("dtag", [32, 2, T], F32, dma=True)
        dma("sp", dtag.t[:], DTAG.ap[0:64, 0:T].rearrange("(a h) t -> h a t", a=2), dtag.sem, reads=[DTAG.tr], writes=[dtag.tr])
        op("act", lambda e: e.activation(out=dtag.t[:, 1, :], in_=dtag.t[:, 1, :], func=AF.Exp), reads=[dtag.tr], writes=[dtag.tr])
        ex = ph.sb("ex", [128, 16, 2 * T], F32)
        for m in range(16):
            ps = S.ps()
            op("pe", lambda e: e.matmul(ps.t[:, 0:2 * T], selh2.t[:, m, :], dtag.t[:].rearrange("h a t -> h (a t)"), start=True, stop=True), reads=[selh2.tr, dtag.tr], writes=[ps.tr])
            op("act", lambda e: e.activation(out=ex.t[:, m, :], in_=ps.t[:, 0:2 * T], func=AF.Copy), reads=[ps.tr], writes=[ex.tr])
        xdt = ph.sb("xdt", [128, 16, T], F32)
        op("dve", lambda e: e.tensor_tensor(out=xdt.t[:], in0=xs.t[:], in1=ex.t[:, :, 0:T], op=ALU.mult), reads=[xs.tr, ex.tr], writes=[xdt.tr])
        ycol = ph.sb("ycol", [128, 16, T], F32, dma=True)
        op("dve", lambda e: e.memset(ycol.t[:], 0.0), writes=[ycol.tr])
        for s in range(NS):
            h0 = ph.rot("h0", 2, [128, 16, 128], F32)
            dma("sp", h0.t[:], st_h.ap[l, s].rearrange("(m p) n -> p m n", p=128), h0.sem, reads=[st_h.tr], writes=[h0.tr])
            hn = ph.rot("hn", 2, [128, 16, 128], F32)
            for g in range(8):
                bb = []
                for which in range(2):
                    dg = ph.rot("dgs", 3, [128, 128], F32)
                    op("dve", lambda e: e.tensor_scalar(out=dg.t[:], in0=ident.t[:], scalar1=bcs.t[:, which * 8 + g, s:s + 1], scalar2=None, op0=ALU.mult), reads=[ident.tr, bcs.tr], writes=[dg.tr])
                    pb = S.ps()
                    op("pe", lambda e: e.matmul(pb.t[:, 0:128], ones.t[:], dg.t[:], start=True, stop=True), reads=[ones.tr, dg.tr], writes=[pb.tr])
                    bb.append(pb)
                cbc = ph.rot("cbc", 2, [128, 128], F32)
                op("act", lambda e: e.activation(out=cbc.t[:], in_=bb[1].t[:, 0:128], func=AF.Copy), reads=[bb[1].tr], writes=[cbc.tr])
                for q in range(2):
                    m = g * 2 + q
                    bx = ph.rot("bx", 3, [128, 128], F32)
                    op("act", lambda e: e.activation(out=bx.t[:], in_=bb[0].t[:, 0:128], func=AF.Copy, scale=xdt.t[:, m, s:s + 1]), reads=[bb[0].tr, xdt.tr], writes=[bx.tr])
                    op("dve", lambda e: e.scalar_tensor_tensor(out=hn.t[:, m, :], in0=h0.t[:, m, :], scalar=ex.t[:, m, T + s:T + s + 1], in1=bx.t[:], op0=ALU.mult, op1=ALU.add), reads=[h0.tr, ex.tr, bx.tr], writes=[hn.tr])
                    junk = ph.rot("junk", 2, [128, 128], F32)
                    op("dve", lambda e: e.tensor_tensor(out=junk.t[:], in0=hn.t[:, m, :], in1=cbc.t[:], op=ALU.mult), reads=[hn.tr, cbc.tr], writes=[junk.tr])
                    op("dve", lambda e: e.tensor_reduce(out=ycol.t[:, m, s:s + 1], in_=junk.t[:], axis=AX.X, op=ALU.add), reads=[junk.tr], writes=[ycol.tr])
            dma("sp", o_sh.ap[l, s].rearrange("(m p) n -> p m n", p=128), hn.t[:], hn.sem, reads=[hn.tr], writes=[o_sh.tr])
        dma("sp", YT.ap[:, 0:T].rearrange("(kc p) t -> p kc t", p=128), ycol.t[:], ycol.sem, reads=[ycol.tr], writes=[YT.tr])
        ph.close()

    def att_sample(l):
        ph = Phase(S)
        T = NS
        sels = ph.sb("sels", [16, 16, 128], F32, dma=True)
        dma("sp", sels.t[:], c_sels.ap.rearrange("k (s p) -> k s p", s=16), sels.sem, writes=[sels.tr])
        tabs = ph.sb("tabs", [128, 12], F32, dma=True)
        dma("sp", tabs.t[:], c_tabs.ap, tabs.sem, writes=[tabs.tr])
        b0 = ph.sb("b0", [128, 12, T], F32, dma=True)
        dma("sp", b0.t[:], c_b0.ap.rearrange("p (h s) -> p h s", h=12), b0.sem, writes=[b0.tr])
        qtok = ph.sb("qtok", [16, 1536], F32, dma=True)
        dma("sp", qtok.t[:], QTOK.ap, qtok.sem, reads=[QTOK.tr], writes=[qtok.tr])
        qf = ph.sb("qf", [128, 12, T], F32, dma=True)
        dma("sp", qf.t[:], QF.ap.rearrange("(h p) t -> p h t", p=128), qf.sem, reads=[QF.tr], writes=[qf.tr])
        kf = ph.sb("kf", [128, 12, T], F32, dma=True)
        dma("sp", kf.t[:], KF.ap.rearrange("(h p) t -> p h t", p=128), kf.sem, reads=[KF.tr], writes=[kf.tr])
        vf = ph.sb("vf", [128, 12, T], F32, dma=True)
        dma("sp", vf.t[:], VF.ap.rearrange("(h p) t -> p h t", p=128), vf.sem, reads=[VF.tr], writes=[vf.tr])
        op("dve", lambda e: e.tensor_tensor(out=kf.t[:], in0=qf.t[:], in1=kf.t[:], op=ALU.mult), reads=[qf.tr, kf.tr], writes=[kf.tr])
        ps0 = S.ps()
        op("pe", lambda e: e.matmul(ps0.t[:, 0:12 * T], ones.t[:], kf.t[:].rearrange("p h t -> p (h t)"), start=True, stop=True), reads=[ones.tr, kf.tr], writes=[ps0.tr])
        p0 = ph.sb("p0", [128, 12, T], F32)
        op("dve", lambda e: e.scalar_tensor_tensor(out=p0.t[:].rearrange("p h t -> p (h t)"), in0=ps0.t[:, 0:12 * T], scalar=SCALE, in1=b0.t[:].rearrange("p h t -> p (h t)"), op0=ALU.mult, op1=ALU.add), reads=[ps0.tr, b0.tr], writes=[p0.tr])
        op("act", lambda e: e.activation(out=p0.t[:], in_=p0.t[:], func=AF.Exp), reads=[p0.tr], writes=[p0.tr])
        num = ph.sb("num", [128, 4, T], F32)
        den = ph.sb("den", [128, 4, T], F32)
        op("dve", lambda e: e.tensor_tensor(out=vf.t[:], in0=vf.t[:], in1=p0.t[:], op=ALU.mult), reads=[vf.tr, p0.tr], writes=[vf.tr])
        op("dve", lambda e: e.tensor_tensor(out=num.t[:], in0=vf.t[:, 0:4, :], in1=vf.t[:, 4:8, :], op=ALU.add), reads=[vf.tr], writes=[num.tr])
        op("dve", lambda e: e.tensor_tensor(out=num.t[:], in0=num.t[:], in1=vf.t[:, 8:12, :], op=ALU.add), reads=[vf.tr, num.tr], writes=[num.tr])
        op("dve", lambda e: e.tensor_tensor(out=den.t[:], in0=p0.t[:, 0:4, :], in1=p0.t[:, 4:8, :], op=ALU.add), reads=[p0.tr], writes=[den.tr])
        op("dve", lambda e: e.tensor_tensor(out=den.t[:], in0=den.t[:], in1=p0.t[:, 8:12, :], op=ALU.add), reads=[p0.tr, den.tr], writes=[den.tr])
        for g, (win, dil) in enumerate(GROUPS):
            pn = pslong
            for s in range(NS):
                kv = ph.rot("kv", 3, [128, 1024], F32)
                dma("sp", kv.t[:], caches[g].ap[l, s, 0:128 * dil:dil, :], kv.sem, reads=[caches[g].tr], writes=[kv.tr])
                pq = S.ps()
                op("pe", lambda e: e.matmul(pq.t[:], sels.t[:, s, :], qtok.t[:, g * 512:(g + 1) * 512], start=True, stop=True), reads=[sels.tr, qtok.tr], writes=[pq.tr])
                pr_ = ph.rot("pr", 2, [128, 512], F32)
                op("dve", lambda e: e.tensor_tensor(out=pr_.t[:], in0=kv.t[:, 0:512], in1=pq.t[:], op=ALU.mult), reads=[kv.tr, pq.tr], writes=[pr_.tr])
                sc = ph.rot("sc", 2, [128, 4], F32)
                op("dve", lambda e: e.tensor_reduce(out=sc.t[:], in_=pr_.t[:].rearrange("p (j d) -> p j d", j=4), axis=AX.X, op=ALU.add), reads=[pr_.tr], writes=[sc.tr])
                op("dve", lambda e: e.scalar_tensor_tensor(out=sc.t[:], in0=sc.t[:], scalar=SCALE, in1=tabs.t[:, g * 4:(g + 1) * 4], op0=ALU.mult, op1=ALU.add), reads=[sc.tr, tabs.tr], writes=[sc.tr])
                pp = ph.rot("pp", 2, [128, 4], F32)
                op("act", lambda e: e.activation(out=pp.t[:], in_=sc.t[:], func=AF.Exp), reads=[sc.tr], writes=[pp.tr])
                for j in range(4):
                    op("pe", lambda e: e.matmul(pn.t[:, s * 4 + j:s * 4 + j + 1], kv.t[:, 512 + j * 128:512 + (j + 1) * 128], pp.t[:, j:j + 1], start=True, stop=True, skip_group_check=True), reads=[kv.tr, pp.tr], writes=[pn.tr])
                op("pe", lambda e: e.matmul(pn.t[:, 64 + s * 4:64 + s * 4 + 4], ones.t[:], pp.t[:], start=True, stop=True, skip_group_check=True), reads=[ones.tr, pp.tr], writes=[pn.tr])
            op("dve", lambda e: e.tensor_tensor(out=num.t[:], in0=num.t[:], in1=pn.t[:, 0:64].rearrange("p (s j) -> p j s", j=4), op=ALU.add), reads=[num.tr, pn.tr], writes=[num.tr])
            op("dve", lambda e: e.tensor_tensor(out=den.t[:], in0=den.t[:], in1=pn.t[:, 64:128].rearrange("p (s j) -> p j s", j=4), op=ALU.add), reads=[den.tr, pn.tr], writes=[den.tr])
        op("dve", lambda e: e.reciprocal(out=den.t[:], in_=den.t[:]), reads=[den.tr], writes=[den.tr])
        yb = ph.sb("yb", [128, 4, T], BF16, dma=True)
        op("dve", lambda e: e.tensor_tensor(out=yb.t[:], in0=num.t[:], in1=den.t[:], op=ALU.mult), reads=[num.tr, den.tr], writes=[yb.tr])
        dma("sp", YATT.ap[:, 0:T].rearrange("(h p) t -> p h t", p=128), yb.t[:], yb.sem, reads=[yb.tr], writes=[YATT.tr])
        ph.close()

    def dense_layer_tail(l, T, xsrc, xc0, xdst, dc0):
        dense([(YSC, w_sc.ap[l], 16), (YSSM, w_ssm.ap[l], 16), (YATT, w_att.ap[l], 4)], [(0, D)], T, epi_merge, G=128)
        dense([(MRG, w_out.ap[l], 16)], [(0, D)], T, make_epi_res(xsrc, xc0, X1T, 0))
        norm_phase(X1T, 0, T, P_N2, l, HT)
        dense([(HT, w_up.ap[l], 16)], [(0, 4 * D)], T, epi_up)
        dense([(AT, w_dn.ap[l], 64)], [(0, D)], T, make_epi_res(X1T, 0, xdst, dc0), G=128)

    WIN_RANGES = DBG_RANGES or [(0, DT0), (DT0, 32), (QKV0, 4608)]

    try:
        for b in range(NB):
            in_transpose_src = DT_.__new__(DT_)
            in_transpose_src.ap = x_p.ap[b * TB:(b + 1) * TB, :]
            in_transpose_src.tr = x_p.tr
            in_transpose(in_transpose_src, TB, XT[0], b * TB)
        in_transpose(x_s, NS, XS[0], 0)
        if DBG_MODE == 'upA':
            dense([(HT, w_up.ap[0], 16)], [(0, 4 * D)], TB, epi_up)
            dense([(AT, w_dn.ap[0], 16)], [(0, D)], TB, make_epi_res(XT[0], 0, XT[1], 0))
            raise StopBuild()
        if DBG_MODE == 'upB':
            dense([(HT, w_up.ap[0], 16)], [(0, 2048)], TB, epi_up)
            dense([(AT, w_dn.ap[0], 64)], [(0, 512)], TB, make_epi_res(XT[0], 0, XT[1], 0), G=128)
            raise StopBuild()
        if DBG_MODE == 'updown':
            dense([(HT, w_up.ap[0], 16)], [(0, 4 * D)], TB, epi_up)
            dense([(AT, w_dn.ap[0], 64)], [(0, D)], TB, make_epi_res(XT[0], 0, XT[1], 0))
            raise StopBuild()
        if DBG_MODE == 'down2':
            dense([(AT, w_dn.ap[0], 64)], [(0, D)], TB, make_epi_res(XT[0], 0, XT[1], 0))
            dense([(AT, w_dn.ap[0], 64)], [(0, D)], TB, make_epi_res(XT[0], 0, XT[1], 0))
            raise StopBuild()
        if DBG_MODE == 'down':
            dense([(AT, w_dn.ap[0], 64)], [(0, DBG_N)], TB, make_epi_res(XT[0], 0, XT[1], 0))
            raise StopBuild()

        for l in range(DEPTH):
            for t_ in (halo_sc, halo_cv, Hst):
                op("dve", lambda e: e.memset(t_.t[:], 0.0), writes=[t_.tr])
            for b in range(NB):
                tok0 = b * TB
                norm_phase(XT[l], tok0, TB, P_N1, l, HT)
                for rg in WIN_RANGES:
                    dense([(HT, w_in.ap[l], 16)], [rg], TB, epi_win)
                conv_phase(l, TB, 16, 8192, halo_sc, P_SCW, 3, None, sc_store(l, TB, YSC))
                conv_phase(l, TB, 32, 14336, halo_cv, P_SSW, 4, P_SSB, cv_store(l, TB))
                dt_phase(l, TB, True)
                ssd_phase(l, TB)
                ssm_post(l, TB, YSSM)
                att_prep(l, TB, tok0, False)
                att_phase(l, TB, tok0)
                dense_layer_tail(l, TB, XT[l], tok0, XT[l + 1], tok0)
            halo_out(halo_sc, 16, 2, o_psc.ap[l], o_psc.tr)
            halo_out(halo_cv, 32, 3, o_pcv.ap[l], o_pcv.tr)
            hstate_out(o_ph.ap[l], o_ph.tr)
            if l == 0:
                hs_sc = S.mktile(es, "hs_sc", [128, 16, 2 * NS], F32, "sb", None)
                hs_cv = S.mktile(es, "hs_cv", [128, 32, 3 * NS], F32, "sb", None)
            halo_in(hs_sc, 16, 2, st_sc.ap[l], st_sc.tr, NS)
            halo_in(hs_cv, 32, 3, st_cv.ap[l], st_cv.tr, NS)
            norm_phase(XS[l], 0, NS, P_N1, l, HT)
            for rg in WIN_RANGES:
                dense([(HT, w_in.ap[l], 16)], [rg], NS, epi_win)
            sample_conv(S, nc, l, hs_sc, 16, 2, P_SCW, prm_sb, PROJ, XBC, YSC, o_ssc, True, ident)
            sample_conv(S, nc, l, hs_cv, 32, 3, P_SSW, prm_sb, PROJ, XBC, YSC, o_scv, False, ident)
            dt_phase(l, NS, False)
            ssd_sample(l)
            ssm_post(l, NS, YSSM)
            att_prep(l, NS, 0, True)
            att_sample(l)
            dense_layer_tail(l, NS, XS[l], 0, XS[l + 1], 0)

        for b in range(NB):
            dst = DT_.__new__(DT_)
            dst.ap = y_p.ap[b * TB:(b + 1) * TB, :]
            dst.tr = y_p.tr
            out_transpose(XT[DEPTH], b * TB, TB, dst)
        out_transpose(XS[DEPTH], 0, NS, y_s)
    except StopBuild:
        pass
    S.barrier()
    es.close()
    return nc


def sample_conv(S, nc, l, hs, nch, nh, wcol0, prm_sb, PROJ, XBC, YSC, o_st, is_sc, ident):
    op = S.op
    dma = S.dma
    T = NS
    ntap = nh + 1
    ph = Phase(S)
    pr = prm_sb[l]
    hv = hs.t[:].rearrange("p c (s j) -> p c s j", j=nh)
    newst = ph.sb("newst", [128, nch, T, nh], F32)
    for kc in range(nch):
        u = ph.rot("u", 3, [128, T], F32)
        if is_sc:
            cx = ph.rot("cx", 2, [128, 2, T], F32)
            dma("sp", cx.t[:], PROJ.ap[8192:12288, 0:T].rearrange("(b f) t -> f b t", b=2)[kc * 128:(kc + 1) * 128], cx.sem, reads=[PROJ.tr], writes=[cx.tr])
            op("dve", lambda e: e.tensor_tensor(out=u.t[:], in0=cx.t[:, 0, :], in1=cx.t[:, 1, :], op=ALU.mult), reads=[cx.tr], writes=[u.tr])
        else:
            dma("sp", u.t[:], PROJ.ap[14336 + kc * 128:14336 + (kc + 1) * 128, 0:T], u.sem, reads=[PROJ.tr], writes=[u.tr])
        cv = ph.rot("cv", 3, [128, T], F32)
        w0 = wcol0 + kc * ntap
        op("dve", lambda e: e.tensor_scalar(out=cv.t[:], in0=u.t[:], scalar1=pr.t[:, w0 + nh:w0 + nh + 1], scalar2=None, op0=ALU.mult), reads=[u.tr, pr.tr], writes=[cv.tr])
        for i in range(nh):
            op("dve", lambda e: e.scalar_tensor_tensor(out=cv.t[:], in0=hv[:, kc, :, i], scalar=pr.t[:, w0 + i:w0 + i + 1], in1=cv.t[:], op0=ALU.mult, op1=ALU.add), reads=[hs.tr, cv.tr, pr.tr], writes=[cv.tr])
        for i in range(nh - 1):
            op("act", lambda e: e.activation(out=newst.t[:, kc, :, i], in_=hv[:, kc, :, i + 1], func=AF.Copy), reads=[hs.tr], writes=[newst.tr])
        op("act", lambda e: e.activation(out=newst.t[:, kc, :, nh - 1], in_=u.t[:], func=AF.Copy), reads=[u.tr], writes=[newst.tr])
        if is_sc:
            b = ph.rot("b", 2, [128, T], F32)
            dma("sp", b.t[:], PROJ.ap[6144 + kc * 128:6144 + (kc + 1) * 128, 0:T], b.sem, reads=[PROJ.tr], writes=[b.tr])
            st = ph.rot("st", 3, [128, T], BF16)
            op("dve", lambda e: e.tensor_tensor(out=st.t[:], in0=cv.t[:], in1=b.t[:], op=ALU.mult), reads=[cv.tr, b.tr], writes=[st.tr])
            dma("sp", YSC.ap[kc * 128:(kc + 1) * 128, 0:T], st.t[:], st.sem, reads=[st.tr], writes=[YSC.tr])
        else:
            st = ph.rot("st", 3, [128, T], F32)
            op("act", lambda e: e.activation(out=st.t[:], in_=cv.t[:], func=AF.Silu, bias=pr.t[:, 240 + kc:240 + kc + 1]), reads=[cv.tr, pr.tr], writes=[st.tr])
            dma("sp", XBC.ap[kc * 128:(kc + 1) * 128, 0:T], st.t[:], st.sem, reads=[st.tr], writes=[XBC.tr])
    rows = T * nh
    o = ph.sb("o", [rows, nch * 128], F32, dma=True)
    for kc in range(nch):
        ps = S.ps()
        op("pe", lambda e: e.transpose(ps.t[0:rows, 0:128], newst.t[:, kc, :, :].rearrange("p s j -> p (s j)"), ident.t[:]), reads=[newst.tr, ident.tr], writes=[ps.tr])
        op("dve", lambda e: e.tensor_copy(out=o.t[0:rows, kc * 128:(kc + 1) * 128], in_=ps.t[0:rows, 0:128]), reads=[ps.tr], writes=[o.tr])
    dma("sp", o_st.ap[l], o.t[:], o.sem, reads=[o.tr], writes=[o_st.tr])
    ph.close()


def _t5_bucket(d):
    d = np.asarray(d, np.int64)
    max_exact = 16
    df = np.maximum(d, 1).astype(np.float32)
    large = max_exact + (np.log(df / max_exact) / math.log(2048 / max_exact) * (32 - max_exact)).astype(np.int32)
    large = np.minimum(large, 31)
    return np.where(d < max_exact, d, large)


def _consts(rel_bias):
    c = {}
    c["c_ident"] = np.eye(128, dtype=np.float32)
    c["c_ones"] = np.ones((128, 128), np.float32)
    j = np.arange(128)[:, None]
    i = np.arange(128)[None, :]
    m = np.where(i < j, NEG, 0.0).astype(np.float32)
    c["c_mask4"] = np.tile(m, (1, 4))
    selh = np.zeros((32, 32, 128), np.float32)
    for h in range(32):
        selh[h, h, :] = 1.0
    c["c_selh"] = selh.reshape(32, -1)
    selh2 = np.zeros((32, 16, 128), np.float32)
    for mm in range(16):
        selh2[2 * mm, mm, 0:64] = 1.0
        selh2[2 * mm + 1, mm, 64:128] = 1.0
    c["c_selh2"] = selh2.reshape(32, -1)
    sels = np.zeros((16, 16, 128), np.float32)
    for s in range(16):
        sels[s, s, :] = 1.0
    c["c_sels"] = sels.reshape(16, -1)
    rb = np.concatenate([np.asarray(rel_bias, np.float32), np.full((1, 12), NEG, np.float32)], axis=0)
    kj = np.arange(256)[:, None]
    qi = np.arange(128)[None, :]
    sdist = qi + 128 - kj
    band = (sdist >= 0) & (sdist <= 128)
    tab = np.zeros((128, 3, 4, 2, 128), np.float32)
    tabs = np.zeros((128, 3, 4), np.float32)
    b0 = np.zeros((128, 12, NS), np.float32)
    for g, (w, dil) in enumerate(GROUPS):
        bidx = _t5_bucket(np.clip(sdist, 0, 128) * dil)
        bidx = np.where(band, bidx, 32)
        steps = 128 - np.arange(128)
        sidx = _t5_bucket(steps * dil)
        for jh in range(4):
            t = rb[bidx, g * 4 + jh]
            tab[:, g, jh, 0, :] = t[0:128]
            tab[:, g, jh, 1, :] = t[128:256]
            tabs[:, g, jh] = rb[sidx, g * 4 + jh]
            b0[:, g * 4 + jh, :] = rb[0, g * 4 + jh]
    c["c_tab"] = tab.reshape(128, -1)
    c["c_tabs"] = tabs.reshape(128, -1)
    c["c_b0"] = b0.reshape(128, -1)
    return c


def _pack_prm(inp, l):
    p = np.zeros((128, NPRM), np.float32)

    def col(v, n):
        return np.asarray(v, np.float32).reshape(n, 128).T

    p[:, 0:16] = col(inp["norm1_g"][l], 16)
    p[:, 16:32] = col(inp["norm2_g"][l], 16)
    p[:, 32:48] = col(inp["ssm_norm_g"][l], 16)
    p[:, 48:64] = col(np.repeat(np.asarray(inp["ssm_D"][l], np.float32), 64), 16)
    scw = np.asarray(inp["sc_conv_w"][l], np.float32)
    p[:, 64:112] = scw.reshape(3, 16, 128).transpose(2, 1, 0).reshape(128, 48)
    ssw = np.asarray(inp["ssm_conv_w"][l], np.float32)
    p[:, 112:240] = ssw.reshape(4, 32, 128).transpose(2, 1, 0).reshape(128, 128)
    p[:, 240:272] = col(inp["ssm_conv_b"][l], 32)
    p[:, 272] = np.asarray(inp["q_norm_g"][l], np.float32)
    p[:, 273] = np.asarray(inp["k_norm_g"][l], np.float32)
    p[0:32, 274] = np.asarray(inp["ssm_dt_bias"][l], np.float32)
    p[0:32, 275] = np.asarray(inp["ssm_A_log"][l], np.float32)
    return p


_NC_CACHE = {}


def kernel(**inp):
    x_prompt = np.asarray(inp["x_prompt"], np.float32)
    B, SEQ, _ = x_prompt.shape
    x_sample = np.asarray(inp["x_sample"], np.float32)
    DB = x_sample.shape[0]
    ncore = 8
    assert DB == ncore * NS
    if SEQ not in _NC_CACHE:
        _NC_CACHE[SEQ] = build(SEQ)
    nc = _NC_CACHE[SEQ]
    consts = _consts(inp["rel_bias"])
    prm = np.stack([_pack_prm(inp, l) for l in range(DEPTH)])
    shared = dict(consts)
    shared["prm"] = prm
    shared["w_in"] = np.asarray(inp["w_in"], np.float32)
    shared["w_sc"] = np.asarray(inp["w_br_sc"], np.float32)
    shared["w_ssm"] = np.asarray(inp["w_br_ssm"], np.float32)
    shared["w_att"] = np.asarray(inp["w_br_att"], np.float32)
    shared["w_out"] = np.asarray(inp["w_out"], np.float32)
    shared["w_up"] = np.asarray(inp["w_up"], np.float32)
    shared["w_dn"] = np.asarray(inp["w_down"], np.float32)
    ssc = np.asarray(inp["state_sc_conv"], np.float32)
    scv = np.asarray(inp["state_ssm_conv"], np.float32)
    sh = np.asarray(inp["state_ssm"], np.float32)
    cas = [np.asarray(inp[k], np.float32) for k in ("cache_kv_w128", "cache_kv_w512", "cache_kv_w2048")]
    in_maps = []
    for c in range(ncore):
        sl = slice(c * NS, (c + 1) * NS)
        m = dict(shared)
        m["x_p"] = np.ascontiguousarray(x_prompt[c % B])
        m["x_s"] = np.ascontiguousarray(x_sample[sl, 0, :])
        m["st_sc"] = np.ascontiguousarray(ssc[:, sl].reshape(DEPTH, NS * 2, D))
        m["st_cv"] = np.ascontiguousarray(scv[:, sl].reshape(DEPTH, NS * 3, 4096))
        m["st_h"] = np.ascontiguousarray(sh[:, sl].reshape(DEPTH, NS, 2048, 128))
        for g in range(3):
            a = cas[g][:, sl]
            m["ca%d" % g] = np.ascontiguousarray(a.reshape(DEPTH, NS, a.shape[2], 1024))
        in_maps.append(m)
    res = run_bass_kernel_spmd(nc, in_maps, core_ids=list(range(ncore))).results
    KEEP = [min(w, SEQ) for w, dl in GROUPS]
    y_prompt = np.stack([res[b]["y_p"] for b in range(B)])
    y_sample = np.concatenate([res[c]["y_s"] for c in range(ncore)])[:, None, :]
    p_sc = np.stack([res[b]["o_psc"] for b in range(B)], axis=1)
    p_cv = np.stack([res[b]["o_pcv"] for b in range(B)], axis=1)
    p_h = np.stack([res[b]["o_ph"].reshape(DEPTH, 32, 64, 128) for b in range(B)], axis=1)
    p_kv = [np.stack([res[b]["o_pkv%d" % g].reshape(DEPTH, KEEP[g], 2, 4, 128) for b in range(B)], axis=1) for g in range(3)]
    s_sc = np.concatenate([res[c]["o_ssc"].reshape(DEPTH, NS, 2, D) for c in range(ncore)], axis=1)
    s_cv = np.concatenate([res[c]["o_scv"].reshape(DEPTH, NS, 3, 4096) for c in range(ncore)], axis=1)
    s_h = np.concatenate([res[c]["o_sh"].reshape(DEPTH, NS, 32, 64, 128) for c in range(ncore)], axis=1)
    s_kv = [np.concatenate([res[c]["o_skv%d" % g].reshape(DEPTH, NS, 1, 2, 4, 128) for c in range(ncore)], axis=1) for g in range(3)]
    outs = (y_prompt, y_sample, p_sc, p_cv, p_h, p_kv[0], p_kv[1], p_kv[2], s_sc, s_cv, s_h, s_kv[0], s_kv[1], s_kv[2])
    return tuple(np.ascontiguousarray(o, dtype=np.float32) for o in outs)
```

```python
import contextlib
import math
import numpy as np
import concourse.bass as bass
import concourse.mybir as mybir
from concourse.bass_utils import run_bass_kernel_spmd

F32 = mybir.dt.float32
BF16 = mybir.dt.bfloat16
AF = mybir.ActivationFunctionType
ALU = mybir.AluOpType
AX = mybir.AxisListType

D = 2048
DEPTH = 2
NS = 16
TB = 512
N_IN = 23072
PROJ_ROWS = 23168
QKV0 = 18464
QKVR = 18560
DT0 = 18432
EPS = 1e-6
SCALE = 1.0 / math.sqrt(128.0)
GROUPS = ((128, 1), (512, 4), (2048, 16))
NEG = -30000.0
NPRM = 280


KSTOP = 0
DBG_KIND = 'Internal'
DBG_MODE = None
DBG_N = 2048
DBG_RANGES = None


class StopBuild(Exception):
    pass


class Sem:
    __slots__ = ("h", "v")


class TT:
    __slots__ = ("w", "r", "ep")

    def __init__(self):
        self.w = {}
        self.r = {}
        self.ep = 0


class Tile:
    __slots__ = ("t", "tr", "sem")


class Sched:
    def __init__(self, nc, es, ndma=64):
        self.nc = nc
        self.es = es
        self.eng = {"pe": nc.tensor, "act": nc.scalar, "dve": nc.vector, "pool": nc.gpsimd, "sp": nc.sync}
        self.esem = {k: self._mk("e_" + k) for k in self.eng}
        self.free = [self._mk("d%d" % i) for i in range(ndma)]
        self.all_d = list(self.free)
        self.known = {k: {} for k in self.eng}
        self.known["pe"][self.esem["pe"]] = 1 << 60
        self.ep = 0
        self.uid = 0
        self.psb = []
        self.psi = 0
        self.nphase = 0

    def _mk(self, name):
        m = Sem()
        m.h = self.es.enter_context(self.nc.semaphore(name))
        m.v = 0
        return m

    def wait(self, e, sem, val):
        if val <= 0:
            return
        kn = self.known[e]
        if kn.get(sem, 0) >= val:
            return
        self.eng[e].wait_ge(sem.h, val)
        kn[sem] = val

    def _fresh(self, t):
        if t.ep != self.ep:
            t.w = {}
            t.r = {}
            t.ep = self.ep

    def deps(self, e, reads, writes):
        for t in reads:
            self._fresh(t)
            for sem, v in t.w.items():
                self.wait(e, sem, v)
        for t in writes:
            self._fresh(t)
            for sem, v in t.w.items():
                self.wait(e, sem, v)
            for sem, v in t.r.items():
                self.wait(e, sem, v)

    def op(self, e, fn, reads=(), writes=()):
        self.deps(e, reads, writes)
        ins = fn(self.eng[e])
        m = self.esem[e]
        m.v += 1
        ins.then_inc(m.h, 1)
        for t in reads:
            t.r[m] = m.v
        for t in writes:
            t.w[m] = m.v

    def dma(self, q, out, in_, sem, reads=(), writes=(), **kw):
        self.deps(q, reads, writes)
        self.wait(q, sem, sem.v)
        ins = self.eng[q].dma_start(out=out, in_=in_, **kw)
        sem.v += 16
        ins.then_inc(sem.h, 16)
        for t in reads:
            t.r[sem] = sem.v
        for t in writes:
            t.w[sem] = sem.v

    def barrier(self):
        e0 = "sp"
        for k, m in self.esem.items():
            if k != e0:
                self.wait(e0, m, m.v)
        for m in self.all_d:
            self.wait(e0, m, m.v)
        ins = self.eng[e0].nop()
        m0 = self.esem[e0]
        m0.v += 1
        ins.then_inc(m0.h, 1)
        for k in self.eng:
            if k != e0:
                self.eng[k].wait_ge(m0.h, m0.v)
            kn = self.known[k]
            for m in self.esem.values():
                if kn.get(m, 0) < m.v:
                    kn[m] = m.v
            for m in self.all_d:
                kn[m] = m.v
        self.ep += 1

    def ps(self):
        p = self.psb[self.psi % len(self.psb)]
        self.psi += 1
        return p

    def mktile(self, es, name, shape, dt, space="sb", sem=None):
        self.uid += 1
        t = Tile()
        nm = "%s_%d" % (name, self.uid)
        if space == "sb":
            t.t = es.enter_context(self.nc.sbuf_tensor(nm, shape, dt))
        else:
            t.t = es.enter_context(self.nc.psum_tensor(nm, shape, dt))
        t.tr = TT()
        t.sem = sem
        return t


class Phase:
    def __init__(self, S):
        self.S = S
        self.es = contextlib.ExitStack()
        self.sems = []
        self.rots = {}

    def sb(self, name, shape, dt, dma=False):
        sem = None
        if dma:
            sem = self.S.free.pop()
            self.sems.append(sem)
        return self.S.mktile(self.es, name, shape, dt, "sb", sem)

    def rot(self, name, n, shape, dt):
        if name not in self.rots:
            self.rots[name] = [[self.sb(name + str(i), shape, dt, dma=True) for i in range(n)], 0]
        r = self.rots[name]
        t = r[0][r[1] % n]
        r[1] += 1
        return t

    def close(self):
        self.S.barrier()
        self.es.close()
        self.S.free.extend(self.sems)
        self.S.nphase += 1
        if KSTOP and self.S.nphase >= KSTOP:
            raise StopBuild()


class DT_:
    def __init__(self, nc, name, shape, dt, kind="Internal"):
        self.ap = nc.dram_tensor(name, shape, dt, kind=kind).ap()
        self.tr = TT()


def build(SEQ):
    NB = SEQ // TB
    nc = bass.Bass("TRN2", target_bir_lowering=False)
    es = contextlib.ExitStack()
    S = Sched(nc, es)
    op = S.op
    dma = S.dma

    def din(name, shape, dt=F32):
        return DT_(nc, name, shape, dt, "ExternalInput")

    def dout(name, shape, dt=F32):
        return DT_(nc, name, shape, dt, "ExternalOutput")

    def dscr(name, shape, dt=F32):
        return DT_(nc, name, shape, dt, "Internal")

    x_p = din("x_p", [SEQ, D])
    x_s = din("x_s", [NS, D])
    st_sc = din("st_sc", [DEPTH, NS * 2, D])
    st_cv = din("st_cv", [DEPTH, NS * 3, 4096])
    st_h = din("st_h", [DEPTH, NS, 2048, 128])
    caches = [din("ca%d" % g, [DEPTH, NS, min(w, 2048), 1024]) for g, (w, dl) in enumerate(GROUPS)]
    w_in = din("w_in", [DEPTH, D, N_IN])
    w_sc = din("w_sc", [DEPTH, D, D])
    w_ssm = din("w_ssm", [DEPTH, D, D])
    w_att = din("w_att", [DEPTH, 512, D])
    w_out = din("w_out", [DEPTH, D, D])
    w_up = din("w_up", [DEPTH, D, 4 * D])
    w_dn = din("w_dn", [DEPTH, 4 * D, D])
    prm = din("prm", [DEPTH, 128, NPRM])
    c_ident = din("c_ident", [128, 128])
    c_ones = din("c_ones", [128, 128])
    c_mask4 = din("c_mask4", [128, 512])
    c_selh = din("c_selh", [32, 32 * 128])
    c_selh2 = din("c_selh2", [32, 16 * 128])
    c_sels = din("c_sels", [16, 16 * 128])
    c_tab = din("c_tab", [128, 24 * 128])
    c_tabs = din("c_tabs", [128, 12])
    c_b0 = din("c_b0", [128, 12 * NS])
    y_p = dout("y_p", [SEQ, D])
    y_s = dout("y_s", [NS, D])
    o_psc = dout("o_psc", [DEPTH, 2, D])
    o_pcv = dout("o_pcv", [DEPTH, 3, 4096])
    o_ph = dout("o_ph", [DEPTH, 2048, 128])
    KEEP = [min(w, SEQ) for w, dl in GROUPS]
    o_pkv = [dout("o_pkv%d" % g, [DEPTH, KEEP[g], 1024]) for g in range(3)]
    o_ssc = dout("o_ssc", [DEPTH, NS * 2, D])
    o_scv = dout("o_scv", [DEPTH, NS * 3, 4096])
    o_sh = dout("o_sh", [DEPTH, NS, 2048, 128])
    o_skv = [dout("o_skv%d" % g, [DEPTH, NS, 1024]) for g in range(3)]
    XT = [dscr("XT%d" % i, [D, SEQ]) for i in range(DEPTH + 1)]
    XS = [dscr("XS%d" % i, [D, NS]) for i in range(DEPTH + 1)]
    X1T = dscr("X1T", [D, TB])
    HT = dscr("HT", [D, TB], BF16)
    PROJ = dscr("PROJ", [PROJ_ROWS, TB])
    XBC = dscr("XBC", [4096, TB])
    DTAG = dscr("DTAG", [64, TB])
    YT = dscr("YT", [D, TB])
    YSC = DT_(nc, "YSC", [D, TB], BF16, DBG_KIND)
    YSSM = DT_(nc, "YSSM", [D, TB], BF16, DBG_KIND)
    YATT = DT_(nc, "YATT", [512, TB], BF16, DBG_KIND)
    MRG = dscr("MRG", [D, TB], BF16)
    AT = dscr("AT", [4 * D, TB], BF16)
    QT = dscr("QT", [1536, TB], BF16)
    QF = dscr("QF", [1536, NS])
    KF = dscr("KF", [1536, NS])
    VF = dscr("VF", [1536, NS])
    QTOK = dscr("QTOK", [NS, 1536])
    KT = [dscr("KT%d" % i, [1536, SEQ], BF16) for i in range(DEPTH)]
    VV = [dscr("VV%d" % i, [SEQ, 1536], BF16) for i in range(DEPTH)]

    def ptile(name, shape, dt=F32):
        t = S.mktile(es, name, shape, dt, "sb", S.free.pop())
        return t

    ident = ptile("ident", [128, 128])
    ones = ptile("ones", [128, 128])
    prm_sb = [ptile("prm%d" % l, [128, NPRM]) for l in range(DEPTH)]
    halo_sc = ptile("halo_sc", [128, 16, 2])
    halo_cv = ptile("halo_cv", [128, 32, 3])
    Hst = ptile("Hst", [128, 2048])
    zero1 = ptile("zero1", [128, 1])
    nega = ptile("nega", [32, 1])
    S.psb = [S.mktile(es, "psb%d" % i, [128, 512], F32, "ps") for i in range(7)]
    pslong = S.mktile(es, "pslong", [128, 512], F32, "ps")

    dma("sp", ident.t[:], c_ident.ap, ident.sem, writes=[ident.tr])
    dma("sp", ones.t[:], c_ones.ap, ones.sem, writes=[ones.tr])
    for l in range(DEPTH):
        dma("sp", prm_sb[l].t[:], prm.ap[l], prm_sb[l].sem, writes=[prm_sb[l].tr])
    op("dve", lambda e: e.memset(zero1.t[:], 0.0), writes=[zero1.tr])
    S.barrier()

    P_N1, P_N2, P_SNG, P_DCOL, P_SCW, P_SSW, P_SSB, P_QG, P_KG, P_DTB, P_ALOG = 0, 16, 32, 48, 64, 112, 240, 272, 273, 274, 275

    def transpose_to(ph, src_ap, m, n, dst_ap, dst_tr, src_tr, eng="dve"):
        ps = S.ps()
        op("pe", lambda e: e.transpose(ps.t[0:n, 0:m], src_ap, ident.t[0:m, 0:m]), reads=[src_tr, ident.tr], writes=[ps.tr])
        if eng == "dve":
            op("dve", lambda e: e.tensor_copy(out=dst_ap, in_=ps.t[0:n, 0:m]), reads=[ps.tr], writes=[dst_tr])
        else:
            op("act", lambda e: e.activation(out=dst_ap, in_=ps.t[0:n, 0:m], func=AF.Copy), reads=[ps.tr], writes=[dst_tr])

    def rstd_from_ps(ph, ps, np_, T, inv_n, tagn):
        r1 = ph.rot("r1" + tagn, 2, [128, T], F32)
        op("dve", lambda e: e.tensor_scalar(out=r1.t[0:np_, :], in0=ps.t[0:np_, 0:T], scalar1=inv_n, scalar2=EPS, op0=ALU.mult, op1=ALU.add), reads=[ps.tr], writes=[r1.tr])
        op("act", lambda e: e.activation(out=r1.t[0:np_, :], in_=r1.t[0:np_, :], func=AF.Sqrt), reads=[r1.tr], writes=[r1.tr])
        r2 = ph.rot("r2" + tagn, 2, [128, T], F32)
        op("dve", lambda e: e.reciprocal(out=r2.t[0:np_, :], in_=r1.t[0:np_, :]), reads=[r1.tr], writes=[r2.tr])
        return r2

    def in_transpose(src, T, dst, c0):
        ph = Phase(S)
        for t0 in range(0, T, 128):
            m = min(128, T - t0)
            xt = ph.rot("xt", 2, [128, D], F32)
            dma("sp", xt.t[0:m, :], src.ap[t0:t0 + m, :], xt.sem, reads=[src.tr], writes=[xt.tr])
            o = ph.rot("o", 2, [128, 16, 128], F32)
            for kc in range(16):
                transpose_to(ph, xt.t[0:m, kc * 128:(kc + 1) * 128], m, 128, o.t[:, kc, 0:m], o.tr, xt.tr, "dve" if kc % 2 else "act")
            dma("sp", dst.ap[:, c0 + t0:c0 + t0 + m].rearrange("(kc p) t -> p kc t", p=128), o.t[:, :, 0:m], o.sem, reads=[o.tr], writes=[dst.tr])
        ph.close()

    def out_transpose(src, c0, T, dst):
        ph = Phase(S)
        for t0 in range(0, T, 128):
            m = min(128, T - t0)
            xt = ph.rot("xt", 2, [128, 16, 128], F32)
            dma("sp", xt.t[:, :, 0:m], src.ap[:, c0 + t0:c0 + t0 + m].rearrange("(kc p) t -> p kc t", p=128), xt.sem, reads=[src.tr], writes=[xt.tr])
            o = ph.rot("o", 2, [128, D], F32)
            for kc in range(16):
                transpose_to(ph, xt.t[:, kc, 0:m], 128, m, o.t[0:m, kc * 128:(kc + 1) * 128], o.tr, xt.tr, "dve" if kc % 2 else "act")
            dma("sp", dst.ap[t0:t0 + m, :], o.t[0:m, :], o.sem, reads=[o.tr], writes=[dst.tr])
        ph.close()

    def norm_phase(src, c0, T, gcol0, l, dst):
        ph = Phase(S)
        x = ph.sb("x", [128, 16, T], F32, dma=True)
        dma("sp", x.t[:], src.ap[:, c0:c0 + T].rearrange("(kc p) t -> p kc t", p=128), x.sem, reads=[src.tr], writes=[x.tr])
        ps = S.ps()
        for kc in range(16):
            sq = ph.rot("sq", 3, [128, T], F32)
            op("act", lambda e: e.activation(out=sq.t[:], in_=x.t[:, kc, :], func=AF.Square), reads=[x.tr], writes=[sq.tr])
            op("pe", lambda e: e.matmul(ps.t[:, 0:T], ones.t[:], sq.t[:], start=(kc == 0), stop=(kc == 15)), reads=[ones.tr, sq.tr], writes=[ps.tr])
        r = rstd_from_ps(ph, ps, 128, T, 1.0 / D, "n")
        hb = ph.sb("hb", [128, 16, T], BF16, dma=True)
        for kc in range(16):
            op("dve", lambda e: e.scalar_tensor_tensor(out=hb.t[:, kc, :], in0=x.t[:, kc, :], scalar=prm_sb[l].t[:, gcol0 + kc:gcol0 + kc + 1], in1=r.t[:], op0=ALU.mult, op1=ALU.mult), reads=[x.tr, r.tr, prm_sb[l].tr], writes=[hb.tr])
        dma("sp", dst.ap[:, 0:T].rearrange("(kc p) t -> p kc t", p=128), hb.t[:], hb.sem, reads=[hb.tr], writes=[dst.tr])
        ph.close()

    def dense(srcs, ranges, T, epi, G=256):
        ph = Phase(S)
        parts = []
        for i, (A, W, KC) in enumerate(srcs):
            for k0 in range(0, KC, 16):
                kn = min(16, KC - k0)
                a = ph.sb("a%d_%d" % (i, k0), [128, kn, T], BF16, dma=True)
                dma("sp", a.t[:], A.ap[k0 * 128:(k0 + kn) * 128, 0:T].rearrange("(kc p) t -> p kc t", p=128), a.sem, reads=[A.tr], writes=[a.tr])
                parts.append((i, k0, kn, a))
        NP_ = len(parts)
        stg = [[ph.sb("wf%d_%d" % (pi, j), [128, parts[pi][2], G], F32, dma=True) for j in range(2)] for pi in range(NP_)]
        wbf = [[[ph.sb("wb%d_%d_%d" % (pi, j, hh), [128, max(1, parts[pi][2] // 2), G], BF16) for hh in range(2)] for j in range(2)] for pi in range(NP_)]
        groups = []
        for (c0, ncol) in ranges:
            for g0 in range(0, ncol, G):
                groups.append((c0 + g0, min(G, ncol - g0)))

        def load(gi):
            col, gsz = groups[gi]
            for pi, (i, k0, kn, a) in enumerate(parts):
                W = srcs[i][1]
                wf = stg[pi][gi % 2]
                dma("act", wf.t[:, :, 0:gsz], W[k0 * 128:(k0 + kn) * 128, col:col + gsz].rearrange("(kc p) n -> p kc n", p=128), wf.sem, writes=[wf.tr])

        def cast(gi):
            col, gsz = groups[gi]
            for pi in range(NP_):
                wf = stg[pi][gi % 2]
                kn = parts[pi][2]
                hk = kn // 2
                wlo = wbf[pi][gi % 2][0]
                whi = wbf[pi][gi % 2][1]
                op("pool", lambda e: e.tensor_copy(out=wlo.t[:, 0:hk, 0:gsz], in_=wf.t[:, 0:hk, 0:gsz]), reads=[wf.tr], writes=[wlo.tr])
                op("dve", lambda e: e.tensor_copy(out=whi.t[:, 0:kn - hk, 0:gsz], in_=wf.t[:, hk:kn, 0:gsz]), reads=[wf.tr], writes=[whi.tr])

        load(0)
        if len(groups) > 1:
            load(1)
        cast(0)
        for gi, (col, gsz) in enumerate(groups):
            if gi + 1 < len(groups):
                cast(gi + 1)
            if gi + 2 < len(groups):
                load(gi + 2)
            for cc in range(0, gsz, 128):
                csz = min(128, gsz - cc)
                for t0 in range(0, T, 512):
                    tsz = min(512, T - t0)
                    pss = {}
                    for pi, (i, k0, kn, a) in enumerate(parts):
                        if i not in pss:
                            pss[i] = S.ps()
                        ps = pss[i]
                        KC = srcs[i][2]
                        hk = kn // 2
                        for kc in range(kn):
                            wb = wbf[pi][gi % 2][0 if kc < hk else 1]
                            kk = kc if kc < hk else kc - hk
                            op("pe", lambda e: e.matmul(ps.t[0:csz, 0:tsz], wb.t[:, kk, cc:cc + csz], a.t[:, kc, t0:t0 + tsz], start=(k0 + kc == 0), stop=(k0 + kc == KC - 1)), reads=[wb.tr, a.tr], writes=[ps.tr])
                    epi(ph, col + cc, csz, t0, tsz, [pss[i] for i in range(len(srcs))])
        ph.close()

    def epi_win(ph, col, csz, t0, tsz, pss):
        row = col if col < QKV0 else col - QKV0 + QKVR
        st = ph.rot("st", 4, [128, 512], F32)
        if col < 6144:
            op("act", lambda e: e.activation(out=st.t[0:csz, 0:tsz], in_=pss[0].t[0:csz, 0:tsz], func=AF.Sigmoid), reads=[pss[0].tr], writes=[st.tr])
        else:
            op("dve", lambda e: e.tensor_copy(out=st.t[0:csz, 0:tsz], in_=pss[0].t[0:csz, 0:tsz]), reads=[pss[0].tr], writes=[st.tr])
        dma("sp", PROJ.ap[row:row + csz, t0:t0 + tsz], st.t[0:csz, 0:tsz], st.sem, reads=[st.tr], writes=[PROJ.tr])

    def epi_merge(ph, col, csz, t0, tsz, pss):
        g = ph.rot("g", 2, [128, 3, 512], F32)
        dma("sp", g.t[:, :, 0:tsz], PROJ.ap[0:6144, t0:t0 + tsz].rearrange("(b f) t -> f b t", b=3)[col:col + 128], g.sem, reads=[PROJ.tr], writes=[g.tr])
        acc = ph.rot("acc", 2, [128, 512], F32)
        tmp = ph.rot("tmp", 2, [128, 512], F32)
        st = ph.rot("st", 3, [128, 512], BF16)
        op("dve", lambda e: e.tensor_tensor(out=acc.t[:, 0:tsz], in0=pss[0].t[:, 0:tsz], in1=g.t[:, 0, 0:tsz], op=ALU.mult), reads=[pss[0].tr, g.tr], writes=[acc.tr])
        op("dve", lambda e: e.tensor_tensor(out=tmp.t[:, 0:tsz], in0=pss[1].t[:, 0:tsz], in1=g.t[:, 1, 0:tsz], op=ALU.mult), reads=[pss[1].tr, g.tr], writes=[tmp.tr])
        op("dve", lambda e: e.tensor_tensor(out=acc.t[:, 0:tsz], in0=acc.t[:, 0:tsz], in1=tmp.t[:, 0:tsz], op=ALU.add), reads=[acc.tr, tmp.tr], writes=[acc.tr])
        op("dve", lambda e: e.tensor_tensor(out=tmp.t[:, 0:tsz], in0=pss[2].t[:, 0:tsz], in1=g.t[:, 2, 0:tsz], op=ALU.mult), reads=[pss[2].tr, g.tr], writes=[tmp.tr])
        op("dve", lambda e: e.tensor_tensor(out=st.t[:, 0:tsz], in0=acc.t[:, 0:tsz], in1=tmp.t[:, 0:tsz], op=ALU.add), reads=[acc.tr, tmp.tr], writes=[st.tr])
        dma("sp", MRG.ap[col:col + 128, t0:t0 + tsz], st.t[:, 0:tsz], st.sem, reads=[st.tr], writes=[MRG.tr])

    def make_epi_res(xsrc, xc0, dst, dc0):
        def epi(ph, col, csz, t0, tsz, pss):
            xi = ph.rot("xi", 3, [128, 512], F32)
            dma("sp", xi.t[:, 0:tsz], xsrc.ap[col:col + 128, xc0 + t0:xc0 + t0 + tsz], xi.sem, reads=[xsrc.tr], writes=[xi.tr])
            st = ph.rot("st", 3, [128, 512], F32)
            op("dve", lambda e: e.tensor_tensor(out=st.t[:, 0:tsz], in0=pss[0].t[:, 0:tsz], in1=xi.t[:, 0:tsz], op=ALU.add), reads=[pss[0].tr, xi.tr], writes=[st.tr])
            dma("sp", dst.ap[col:col + 128, dc0 + t0:dc0 + t0 + tsz], st.t[:, 0:tsz], st.sem, reads=[st.tr], writes=[dst.tr])
        return epi

    def epi_up(ph, col, csz, t0, tsz, pss):
        r = ph.rot("r", 3, [128, 512], F32)
        op("act", lambda e: e.activation(out=r.t[:, 0:tsz], in_=pss[0].t[:, 0:tsz], func=AF.Relu), reads=[pss[0].tr], writes=[r.tr])
        st = ph.rot("st", 3, [128, 512], BF16)
        op("dve", lambda e: e.tensor_tensor(out=st.t[:, 0:tsz], in0=r.t[:, 0:tsz], in1=r.t[:, 0:tsz], op=ALU.mult), reads=[r.tr], writes=[st.tr])
        dma("sp", AT.ap[col:col + 128, t0:t0 + tsz], st.t[:, 0:tsz], st.sem, reads=[st.tr], writes=[AT.tr])

    def conv_phase(l, T, nch, row0, halo, wcol0, ntap, bias_col, store):
        ph = Phase(S)
        nh = ntap - 1
        for kc in range(nch):
            ue = ph.rot("ue", 3, [128, nh + T], F32)
            r = row0 + kc * 128
            if bias_col is None:
                cx = ph.rot("cx", 2, [128, 2, T], F32)
                dma("sp", cx.t[:], PROJ.ap[8192:12288, 0:T].rearrange("(b f) t -> f b t", b=2)[kc * 128:(kc + 1) * 128], cx.sem, reads=[PROJ.tr], writes=[cx.tr])
                op("dve", lambda e: e.tensor_tensor(out=ue.t[:, nh:nh + T], in0=cx.t[:, 0, :], in1=cx.t[:, 1, :], op=ALU.mult), reads=[cx.tr], writes=[ue.tr])
            else:
                dma("sp", ue.t[:, nh:nh + T], PROJ.ap[r:r + 128, 0:T], ue.sem, reads=[PROJ.tr], writes=[ue.tr])
            op("act", lambda e: e.activation(out=ue.t[:, 0:nh], in_=halo.t[:, kc, :], func=AF.Copy), reads=[halo.tr], writes=[ue.tr])
            cv = ph.rot("cv", 3, [128, T], F32)
            wc = prm_sb[l].t
            op("dve", lambda e: e.tensor_scalar(out=cv.t[:], in0=ue.t[:, 0:T], scalar1=wc[:, wcol0 + kc * ntap:wcol0 + kc * ntap + 1], scalar2=None, op0=ALU.mult), reads=[ue.tr, prm_sb[l].tr], writes=[cv.tr])
            for i in range(1, ntap):
                op("dve", lambda e: e.scalar_tensor_tensor(out=cv.t[:], in0=ue.t[:, i:i + T], scalar=wc[:, wcol0 + kc * ntap + i:wcol0 + kc * ntap + i + 1], in1=cv.t[:], op0=ALU.mult, op1=ALU.add), reads=[ue.tr, cv.tr], writes=[cv.tr])
            op("act", lambda e: e.activation(out=halo.t[:, kc, :], in_=ue.t[:, T:T + nh], func=AF.Copy), reads=[ue.tr], writes=[halo.tr])
            store(ph, kc, ue, cv)
        ph.close()

    def sc_store(l, T, dst):
        def store(ph, kc, ue, cv):
            b = ph.rot("b", 2, [128, T], F32)
            dma("sp", b.t[:], PROJ.ap[6144 + kc * 128:6144 + (kc + 1) * 128, 0:T], b.sem, reads=[PROJ.tr], writes=[b.tr])
            st = ph.rot("st", 3, [128, T], BF16)
            op("dve", lambda e: e.tensor_tensor(out=st.t[:], in0=cv.t[:], in1=b.t[:], op=ALU.mult), reads=[cv.tr, b.tr], writes=[st.tr])
            dma("sp", dst.ap[kc * 128:(kc + 1) * 128, 0:T], st.t[:], st.sem, reads=[st.tr], writes=[dst.tr])
        return store

    def cv_store(l, T):
        def store(ph, kc, ue, cv):
            st = ph.rot("st", 3, [128, T], F32)
            op("act", lambda e: e.activation(out=st.t[:], in_=cv.t[:], func=AF.Silu, bias=prm_sb[l].t[:, P_SSB + kc:P_SSB + kc + 1]), reads=[cv.tr, prm_sb[l].tr], writes=[st.tr])
            dma("sp", XBC.ap[kc * 128:(kc + 1) * 128, 0:T], st.t[:], st.sem, reads=[st.tr], writes=[XBC.tr])
        return store

    def halo_out(halo, nch, nh, dst_ap, dst_tr):
        ph = Phase(S)
        o = ph.sb("o", [nh, nch * 128], F32, dma=True)
        for kc in range(nch):
            transpose_to(ph, halo.t[:, kc, :], 128, nh, o.t[0:nh, kc * 128:(kc + 1) * 128], o.tr, halo.tr)
        dma("sp", dst_ap, o.t[:], o.sem, reads=[o.tr], writes=[dst_tr])
        ph.close()

    def halo_in(halo, nch, nh, src_ap, src_tr, ns):
        ph = Phase(S)
        rows = ns * nh
        x = ph.sb("x", [rows, nch * 128], F32, dma=True)
        dma("sp", x.t[:], src_ap, x.sem, reads=[src_tr], writes=[x.tr])
        for kc in range(nch):
            transpose_to(ph, x.t[0:rows, kc * 128:(kc + 1) * 128], rows, 128, halo.t[:, kc, :], halo.tr, x.tr)
        ph.close()

    def dt_phase(l, T, scan):
        ph = Phase(S)
        d = ph.sb("d", [32, T], F32, dma=True)
        dma("sp", d.t[:], PROJ.ap[DT0:DT0 + 32, 0:T], d.sem, reads=[PROJ.tr], writes=[d.tr])
        pr = prm_sb[l]
        op("act", lambda e: e.activation(out=d.t[:], in_=d.t[:], func=AF.Exp, bias=pr.t[0:32, P_DTB:P_DTB + 1]), reads=[d.tr, pr.tr], writes=[d.tr])
        op("act", lambda e: e.activation(out=d.t[:], in_=d.t[:], func=AF.Ln, bias=1.0), reads=[d.tr], writes=[d.tr])
        op("act", lambda e: e.activation(out=nega.t[:], in_=pr.t[0:32, P_ALOG:P_ALOG + 1], func=AF.Exp), reads=[pr.tr], writes=[nega.tr])
        op("dve", lambda e: e.tensor_scalar(out=nega.t[:], in0=nega.t[:], scalar1=-1.0, scalar2=None, op0=ALU.mult), reads=[nega.tr], writes=[nega.tr])
        a = ph.sb("a", [32, T], F32, dma=True)
        op("dve", lambda e: e.tensor_scalar(out=a.t[:], in0=d.t[:], scalar1=nega.t[:, 0:1], scalar2=None, op0=ALU.mult), reads=[d.tr, nega.tr], writes=[a.tr])
        if scan:
            on = ph.sb("on", [32, T], F32)
            op("dve", lambda e: e.memset(on.t[:], 1.0), writes=[on.tr])
            ag = ph.sb("ag", [32, T], F32, dma=True)
            op("dve", lambda e: e.tensor_tensor_scan(out=ag.t[:], data0=on.t[:], data1=a.t[:], initial=0.0, op0=ALU.mult, op1=ALU.add), reads=[on.tr, a.tr], writes=[ag.tr])
            a = ag
        dma("sp", DTAG.ap[0:32, 0:T], d.t[:], d.sem, reads=[d.tr], writes=[DTAG.tr])
        dma("sp", DTAG.ap[32:64, 0:T], a.t[:], a.sem, reads=[a.tr], writes=[DTAG.tr])
        ph.close()

    def ssd_phase(l, T):
        ph = Phase(S)
        mask4 = ph.sb("mask4", [128, 512], F32, dma=True)
        dma("sp", mask4.t[:], c_mask4.ap, mask4.sem, writes=[mask4.tr])
        bc = ph.sb("bc", [128, 16, T], BF16)
        for hf in range(2):
            bcf = ph.rot("bcf", 2, [128, 8, T], F32)
            dma("sp", bcf.t[:], XBC.ap[2048 + hf * 1024:3072 + hf * 1024, 0:T].rearrange("(kc p) t -> p kc t", p=128), bcf.sem, reads=[XBC.tr], writes=[bcf.tr])
            op("pool", lambda e: e.tensor_copy(out=bc.t[:, hf * 8:(hf + 1) * 8, :], in_=bcf.t[:]), reads=[bcf.tr], writes=[bc.tr])
        dtag = ph.sb("dtag", [32, 2, T], F32, dma=True)
        dma("sp", dtag.t[:], DTAG.ap[0:64, 0:T].rearrange("(a h) t -> h a t", a=2), dtag.sem, reads=[DTAG.tr], writes=[dtag.tr])
        Hb = ph.sb("Hb", [128, 2048], BF16)
        for c in range(T // 128):
            cs = c * 128
            xs = ph.rot("xs", 2, [128, 16, 128], F32)
            dma("sp", xs.t[:], XBC.ap[0:2048, cs:cs + 128].rearrange("(kc p) t -> p kc t", p=128), xs.sem, reads=[XBC.tr], writes=[xs.tr])
            bf = ph.rot("bf", 2, [128, 8, 128], F32)
            dma("sp", bf.t[:], XBC.ap[2048:3072, cs:cs + 128].rearrange("(kc p) t -> p kc t", p=128), bf.sem, reads=[XBC.tr], writes=[bf.tr])
            yt = ph.rot("yt", 2, [128, 16, 128], F32)
            nag0 = ph.rot("nag0", 2, [32, 1], F32)
            if c == 0:
                op("dve", lambda e: e.tensor_copy(out=nag0.t[:], in_=zero1.t[0:32, :]), reads=[zero1.tr], writes=[nag0.tr])
            else:
                op("dve", lambda e: e.tensor_scalar(out=nag0.t[:], in0=dtag.t[:, 1, cs - 1:cs], scalar1=-1.0, scalar2=None, op0=ALU.mult), reads=[dtag.tr], writes=[nag0.tr])
            sm = ph.rot("sm", 2, [32, 4, 128], F32)
            op("dve", lambda e: e.tensor_copy(out=sm.t[:, 0, :], in_=dtag.t[:, 0, cs:cs + 128]), reads=[dtag.tr], writes=[sm.tr])
            op("dve", lambda e: e.tensor_scalar(out=sm.t[:, 1, :], in0=dtag.t[:, 1, cs:cs + 128], scalar1=-1.0, scalar2=None, op0=ALU.mult), reads=[dtag.tr], writes=[sm.tr])
            op("act", lambda e: e.activation(out=sm.t[:, 2, :], in_=dtag.t[:, 1, cs:cs + 128], func=AF.Exp, bias=nag0.t[:, 0:1]), reads=[dtag.tr, nag0.tr], writes=[sm.tr])
            op("act", lambda e: e.activation(out=sm.t[:, 3, :], in_=dtag.t[:, 1, cs:cs + 128], func=AF.Exp, scale=-1.0, bias=dtag.t[:, 1, cs + 127:cs + 128]), reads=[dtag.tr], writes=[sm.tr])
            op("dve", lambda e: e.tensor_tensor(out=sm.t[:, 3, :], in0=sm.t[:, 3, :], in1=dtag.t[:, 0, cs:cs + 128], op=ALU.mult), reads=[sm.tr, dtag.tr], writes=[sm.tr])
            pst = S.ps()
            for q in range(4):
                op("pe", lambda e: e.transpose(pst.t[:, q * 32:(q + 1) * 32], sm.t[:, q, :], ident.t[0:32, 0:32]), reads=[sm.tr, ident.tr], writes=[pst.tr])
            tk = ph.rot("tk", 2, [128, 4, 32], F32)
            op("dve", lambda e: e.tensor_copy(out=tk.t[:].rearrange("p a h -> p (a h)"), in_=pst.t[:, 0:128]), reads=[pst.tr], writes=[tk.tr])
            dg = ph.rot("dg", 2, [32, 32], F32)
            op("dve", lambda e: e.tensor_scalar(out=dg.t[:], in0=ident.t[0:32, 0:32], scalar1=sm.t[:, 2, 127:128], scalar2=None, op0=ALU.mult), reads=[ident.tr, sm.tr], writes=[dg.tr])
            pcd = S.ps()
            op("pe", lambda e: e.matmul(pcd.t[:, 0:32], ones.t[0:32, :], dg.t[:], start=True, stop=True), reads=[ones.tr, dg.tr], writes=[pcd.tr])
            cdb = ph.rot("cdb", 2, [128, 32], F32)
            op("act", lambda e: e.activation(out=cdb.t[:], in_=pcd.t[:, 0:32], func=AF.Copy), reads=[pcd.tr], writes=[cdb.tr])
            xtok = ph.rot("xtok", 1, [128, 2048], F32)
            for g4 in range(4):
                p4 = S.ps()
                for q in range(4):
                    kc = g4 * 4 + q
                    op("pe", lambda e: e.transpose(p4.t[:, q * 128:(q + 1) * 128], xs.t[:, kc, :], ident.t[:]), reads=[xs.tr, ident.tr], writes=[p4.tr])
                if g4 % 2:
                    op("act", lambda e: e.activation(out=xtok.t[:, g4 * 512:(g4 + 1) * 512], in_=p4.t[:], func=AF.Copy), reads=[p4.tr], writes=[xtok.tr])
                else:
                    op("dve", lambda e: e.tensor_copy(out=xtok.t[:, g4 * 512:(g4 + 1) * 512], in_=p4.t[:]), reads=[p4.tr], writes=[xtok.tr])
            btok = ph.rot("btok", 2, [128, 1024], BF16)
            for g4 in range(2):
                p4 = S.ps()
                for q in range(4):
                    kc = g4 * 4 + q
                    op("pe", lambda e: e.transpose(p4.t[:, q * 128:(q + 1) * 128], bf.t[:, kc, :], ident.t[:]), reads=[bf.tr, ident.tr], writes=[p4.tr])
                op("act", lambda e: e.activation(out=btok.t[:, g4 * 512:(g4 + 1) * 512], in_=p4.t[:], func=AF.Copy), reads=[p4.tr], writes=[btok.tr])
            xdt = ph.rot("xdt", 2, [128, 2048], BF16)
            op("dve", lambda e: e.tensor_tensor(out=xdt.t[:].rearrange("p (h d) -> p h d", h=32), in0=xtok.t[:].rearrange("p (h d) -> p h d", h=32), in1=tk.t[:, 0, :].unsqueeze(2).to_broadcast([128, 32, 64]), op=ALU.mult), reads=[xtok.tr, tk.tr], writes=[xdt.tr])
            xw = ph.rot("xw", 2, [128, 2048], BF16)
            op("dve", lambda e: e.tensor_tensor(out=xw.t[:].rearrange("p (h d) -> p h d", h=32), in0=xtok.t[:].rearrange("p (h d) -> p h d", h=32), in1=tk.t[:, 3, :].unsqueeze(2).to_broadcast([128, 32, 64]), op=ALU.mult), reads=[xtok.tr, tk.tr], writes=[xw.tr])
            op("dve", lambda e: e.tensor_copy(out=Hb.t[:], in_=Hst.t[:]), reads=[Hst.tr], writes=[Hb.tr])
            yo = ph.rot("yo", 1, [128, 2048], F32)
            for g2 in range(4):
                pz = S.ps()
                for q in range(2):
                    g = g2 * 2 + q
                    op("pe", lambda e: e.matmul(pz.t[:, q * 256:(q + 1) * 256], bc.t[:, 8 + g, cs:cs + 128], Hb.t[:, g * 256:(g + 1) * 256], start=True, stop=True), reads=[bc.tr, Hb.tr], writes=[pz.tr])
                op("dve", lambda e: e.tensor_tensor(out=yo.t[:, g2 * 512:(g2 + 1) * 512].rearrange("p (h d) -> p h d", h=8), in0=pz.t[:].rearrange("p (h d) -> p h d", h=8), in1=tk.t[:, 2, g2 * 8:(g2 + 1) * 8].unsqueeze(2).to_broadcast([128, 8, 64]), op=ALU.mult), reads=[pz.tr, tk.tr], writes=[yo.tr])
            ysb = ph.rot("ysb", 1, [128, 2048], F32)
            for g in range(8):
                pcb = S.ps()
                op("pe", lambda e: e.matmul(pcb.t[:, 0:128], bc.t[:, g, cs:cs + 128], bc.t[:, 8 + g, cs:cs + 128], start=True, stop=True), reads=[bc.tr], writes=[pcb.tr])
                cb = ph.rot("cb", 2, [128, 128], BF16)
                op("act", lambda e: e.activation(out=cb.t[:], in_=pcb.t[:, 0:128], func=AF.Copy), reads=[pcb.tr], writes=[cb.tr])
                pd = S.ps()
                op("pe", lambda e: e.matmul(pd.t[:], ident.t[:], mask4.t[:], start=True, stop=False), reads=[ident.tr, mask4.tr], writes=[pd.tr])
                rr = ph.rot("rr", 2, [32, 4, 128], F32)
                op("dve", lambda e: e.tensor_tensor(out=rr.t[:], in0=dtag.t[:, 1, cs:cs + 128].unsqueeze(1).to_broadcast([32, 4, 128]), in1=ident.t[0:32, g * 4:(g + 1) * 4].unsqueeze(2).to_broadcast([32, 4, 128]), op=ALU.mult), reads=[dtag.tr, ident.tr], writes=[rr.tr])
                op("pe", lambda e: e.matmul(pd.t[:], ones.t[0:32, :], rr.t[:].rearrange("k q i -> k (q i)"), start=False, stop=True), reads=[ones.tr, rr.tr], writes=[pd.tr])
                dec = ph.rot("dec", 2, [128, 512], BF16)
                for q in range(4):
                    h = g * 4 + q
                    op("act", lambda e: e.activation(out=dec.t[:, q * 128:(q + 1) * 128], in_=pd.t[:, q * 128:(q + 1) * 128], func=AF.Exp, bias=tk.t[:, 1, h:h + 1]), reads=[pd.tr, tk.tr], writes=[dec.tr])
                wg = ph.rot("wg", 2, [128, 512], BF16)
                op("dve", lambda e: e.tensor_tensor(out=wg.t[:].rearrange("p (q i) -> p q i", q=4), in0=dec.t[:].rearrange("p (q i) -> p q i", q=4), in1=cb.t[:].unsqueeze(1).to_broadcast([128, 4, 128]), op=ALU.mult), reads=[dec.tr, cb.tr], writes=[wg.tr])
                py = S.ps()
                for q in range(4):
                    h = g * 4 + q
                    op("pe", lambda e: e.matmul(py.t[:, q * 64:(q + 1) * 64], wg.t[:, q * 128:(q + 1) * 128], xdt.t[:, h * 64:(h + 1) * 64], start=True, stop=True), reads=[wg.tr, xdt.tr], writes=[py.tr])
                op("dve", lambda e: e.tensor_tensor(out=ysb.t[:, g * 256:(g + 1) * 256], in0=py.t[:, 0:256], in1=yo.t[:, g * 256:(g + 1) * 256], op=ALU.add), reads=[py.tr, yo.tr], writes=[ysb.tr])
            for g4 in range(4):
                p4 = S.ps()
                for q in range(4):
                    kc = g4 * 4 + q
                    op("pe", lambda e: e.transpose(p4.t[:, q * 128:(q + 1) * 128], ysb.t[:, kc * 128:(kc + 1) * 128], ident.t[:]), reads=[ysb.tr, ident.tr], writes=[p4.tr])
                op("act", lambda e: e.activation(out=yt.t[:, g4 * 4:(g4 + 1) * 4, :], in_=p4.t[:].rearrange("p (q i) -> p q i", q=4), func=AF.Copy), reads=[p4.tr], writes=[yt.tr])
            dma("sp", YT.ap[:, cs:cs + 128].rearrange("(kc p) t -> p kc t", p=128), yt.t[:], yt.sem, reads=[yt.tr], writes=[YT.tr])
            for g2 in range(4):
                pz = S.ps()
                for q in range(2):
                    g = g2 * 2 + q
                    op("pe", lambda e: e.matmul(pz.t[:, q * 256:(q + 1) * 256], btok.t[:, g * 128:(g + 1) * 128], xw.t[:, g * 256:(g + 1) * 256], start=True, stop=True), reads=[btok.tr, xw.tr], writes=[pz.tr])
                op("dve", lambda e: e.tensor_tensor(out=Hst.t[:, g2 * 512:(g2 + 1) * 512].rearrange("p (h d) -> p h d", h=8), in0=Hst.t[:, g2 * 512:(g2 + 1) * 512].rearrange("p (h d) -> p h d", h=8), in1=cdb.t[:, g2 * 8:(g2 + 1) * 8].unsqueeze(2).to_broadcast([128, 8, 64]), op=ALU.mult), reads=[Hst.tr, cdb.tr], writes=[Hst.tr])
                op("dve", lambda e: e.tensor_tensor(out=Hst.t[:, g2 * 512:(g2 + 1) * 512], in0=Hst.t[:, g2 * 512:(g2 + 1) * 512], in1=pz.t[:], op=ALU.add), reads=[Hst.tr, pz.tr], writes=[Hst.tr])
        ph.close()

    def hstate_out(dst_ap, dst_tr):
        ph = Phase(S)
        for m in range(16):
            o = ph.rot("o", 3, [128, 128], F32)
            transpose_to(ph, Hst.t[:, m * 128:(m + 1) * 128], 128, 128, o.t[:], o.tr, Hst.tr)
            dma("sp", dst_ap[m * 128:(m + 1) * 128, :], o.t[:], o.sem, reads=[o.tr], writes=[dst_tr])
        ph.close()

    def ssm_post(l, T, dst):
        ph = Phase(S)
        pr = prm_sb[l]
        for gp in range(8):
            ts_ = []
            ps = S.ps()
            for q in range(2):
                kc = gp * 2 + q
                y = ph.rot("y", 4, [128, T], F32)
                dma("sp", y.t[:], YT.ap[kc * 128:(kc + 1) * 128, 0:T], y.sem, reads=[YT.tr], writes=[y.tr])
                x = ph.rot("x", 4, [128, T], F32)
                dma("sp", x.t[:], XBC.ap[kc * 128:(kc + 1) * 128, 0:T], x.sem, reads=[XBC.tr], writes=[x.tr])
                z = ph.rot("z", 4, [128, T], F32)
                dma("sp", z.t[:], PROJ.ap[12288 + kc * 128:12288 + (kc + 1) * 128, 0:T], z.sem, reads=[PROJ.tr], writes=[z.tr])
                op("dve", lambda e: e.scalar_tensor_tensor(out=y.t[:], in0=x.t[:], scalar=pr.t[:, P_DCOL + kc:P_DCOL + kc + 1], in1=y.t[:], op0=ALU.mult, op1=ALU.add), reads=[x.tr, y.tr, pr.tr], writes=[y.tr])
                op("act", lambda e: e.activation(out=z.t[:], in_=z.t[:], func=AF.Silu), reads=[z.tr], writes=[z.tr])
                op("dve", lambda e: e.tensor_tensor(out=y.t[:], in0=y.t[:], in1=z.t[:], op=ALU.mult), reads=[y.tr, z.tr], writes=[y.tr])
                op("act", lambda e: e.activation(out=x.t[:], in_=y.t[:], func=AF.Square), reads=[y.tr], writes=[x.tr])
                op("pe", lambda e: e.matmul(ps.t[:, 0:T], ones.t[:], x.t[:], start=(q == 0), stop=(q == 1)), reads=[ones.tr, x.tr], writes=[ps.tr])
                ts_.append((kc, y))
            r = rstd_from_ps(ph, ps, 128, T, 1.0 / 256.0, "g")
            for kc, y in ts_:
                st = ph.rot("st", 3, [128, T], BF16)
                op("dve", lambda e: e.scalar_tensor_tensor(out=st.t[:], in0=y.t[:], scalar=pr.t[:, P_SNG + kc:P_SNG + kc + 1], in1=r.t[:], op0=ALU.mult, op1=ALU.mult), reads=[y.tr, r.tr, pr.tr], writes=[st.tr])
                dma("sp", dst.ap[kc * 128:(kc + 1) * 128, 0:T], st.t[:], st.sem, reads=[st.tr], writes=[dst.tr])
        ph.close()

    def att_prep(l, T, tok0, sample):
        ph = Phase(S)
        pr = prm_sb[l]
        for which in range(2):
            for hd in range(12):
                r0 = QKVR + which * 1536 + hd * 128
                x = ph.rot("x", 3, [128, T], F32)
                dma("sp", x.t[:], PROJ.ap[r0:r0 + 128, 0:T], x.sem, reads=[PROJ.tr], writes=[x.tr])
                sq = ph.rot("sq", 2, [128, T], F32)
                op("act", lambda e: e.activation(out=sq.t[:], in_=x.t[:], func=AF.Square), reads=[x.tr], writes=[sq.tr])
                ps = S.ps()
                op("pe", lambda e: e.matmul(ps.t[:, 0:T], ones.t[:], sq.t[:], start=True, stop=True), reads=[ones.tr, sq.tr], writes=[ps.tr])
                r = rstd_from_ps(ph, ps, 128, T, 1.0 / 128.0, "a")
                gc = P_QG + which
                xn = ph.rot("xn", 3, [128, T], F32)
                op("dve", lambda e: e.scalar_tensor_tensor(out=xn.t[:], in0=x.t[:], scalar=pr.t[:, gc:gc + 1], in1=r.t[:], op0=ALU.mult, op1=ALU.mult), reads=[x.tr, r.tr, pr.tr], writes=[xn.tr])
                if sample:
                    dst = QF if which == 0 else KF
                    dma("sp", dst.ap[hd * 128:(hd + 1) * 128, 0:T], xn.t[:], xn.sem, reads=[xn.tr], writes=[dst.tr])
                else:
                    xb = ph.rot("xb", 3, [128, T], BF16)
                    op("act", lambda e: e.activation(out=xb.t[:], in_=xn.t[:], func=AF.Copy), reads=[xn.tr], writes=[xb.tr])
                    if which == 0:
                        dma("sp", QT.ap[hd * 128:(hd + 1) * 128, 0:T], xb.t[:], xb.sem, reads=[xb.tr], writes=[QT.tr])
                    else:
                        dma("sp", KT[l].ap[hd * 128:(hd + 1) * 128, tok0:tok0 + T], xb.t[:], xb.sem, reads=[xb.tr], writes=[KT[l].tr])
                if sample:
                    o = ph.rot("otk", 3, [128, 128], F32)
                    transpose_to(ph, xn.t[:, 0:T], 128, T, o.t[0:T, :], o.tr, xn.tr)
                    if which == 0:
                        dma("sp", QTOK.ap[:, hd * 128:(hd + 1) * 128], o.t[0:T, :], o.sem, reads=[o.tr], writes=[QTOK.tr])
                    else:
                        g, j = hd // 4, hd % 4
                        dma("sp", o_skv[g].ap[l, :, j * 128:(j + 1) * 128], o.t[0:T, :], o.sem, reads=[o.tr], writes=[o_skv[g].tr])
                elif which == 1:
                    g, j = hd // 4, hd % 4
                    first = SEQ - KEEP[g]
                    for t0 in range(0, T, 128):
                        if tok0 + t0 >= first:
                            o = ph.rot("otk", 3, [128, 128], F32)
                            transpose_to(ph, xn.t[:, t0:t0 + 128], 128, 128, o.t[:], o.tr, xn.tr)
                            rr = tok0 + t0 - first
                            dma("sp", o_pkv[g].ap[l, rr:rr + 128, j * 128:(j + 1) * 128], o.t[:], o.sem, reads=[o.tr], writes=[o_pkv[g].tr])
        for hd in range(12):
            g, j = hd // 4, hd % 4
            r0 = QKVR + 2 * 1536 + hd * 128
            x = ph.rot("x", 3, [128, T], F32)
            dma("sp", x.t[:], PROJ.ap[r0:r0 + 128, 0:T], x.sem, reads=[PROJ.tr], writes=[x.tr])
            if sample:
                dma("sp", VF.ap[hd * 128:(hd + 1) * 128, 0:T], x.t[:], x.sem, reads=[x.tr], writes=[VF.tr])
                o = ph.rot("otk", 3, [128, 128], F32)
                transpose_to(ph, x.t[:, 0:T], 128, T, o.t[0:T, :], o.tr, x.tr)
                dma("sp", o_skv[g].ap[l, :, 512 + j * 128:512 + (j + 1) * 128], o.t[0:T, :], o.sem, reads=[o.tr], writes=[o_skv[g].tr])
            else:
                first = SEQ - KEEP[g]
                for t0 in range(0, T, 128):
                    o = ph.rot("otk", 3, [128, 128], F32)
                    transpose_to(ph, x.t[:, t0:t0 + 128], 128, 128, o.t[:], o.tr, x.tr)
                    ob = ph.rot("ob", 3, [128, 128], BF16)
                    op("act", lambda e: e.activation(out=ob.t[:], in_=o.t[:], func=AF.Copy), reads=[o.tr], writes=[ob.tr])
                    dma("sp", VV[l].ap[tok0 + t0:tok0 + t0 + 128, hd * 128:(hd + 1) * 128], ob.t[:], ob.sem, reads=[ob.tr], writes=[VV[l].tr])
                    if tok0 + t0 >= first:
                        rr = tok0 + t0 - first
                        dma("sp", o_pkv[g].ap[l, rr:rr + 128, 512 + j * 128:512 + (j + 1) * 128], o.t[:], o.sem, reads=[o.tr], writes=[o_pkv[g].tr])
        ph.close()

    def att_phase(l, T, tok0):
        ph = Phase(S)
        tab = ph.sb("tab", [128, 24, 128], F32, dma=True)
        dma("sp", tab.t[:], c_tab.ap.rearrange("p (a i) -> p a i", a=24), tab.sem, writes=[tab.tr])
        q = ph.sb("q", [128, 12, T], BF16, dma=True)
        dma("sp", q.t[:], QT.ap[:, 0:T].rearrange("(h p) t -> p h t", p=128), q.sem, reads=[QT.tr], writes=[q.tr])
        onb = ph.sb("onb", [128, 128], BF16)
        op("dve", lambda e: e.tensor_copy(out=onb.t[:], in_=ones.t[:]), reads=[ones.tr], writes=[onb.tr])
        num = ph.sb("num", [128, 4, T], F32)
        den = ph.sb("den", [128, 4, T], F32)
        first = True
        for g, (win, dil) in enumerate(GROUPS):
            span = 128 * dil
            blk0 = tok0 // span
            w0 = max(0, (blk0 - 1) * span)
            w1 = tok0 + T
            wl = w1 - w0
            kw = ph.sb("kw%d" % g, [128, 4, wl], BF16, dma=True)
            dma("sp", kw.t[:], KT[l].ap[g * 512:(g + 1) * 512, w0:w1].rearrange("(h p) t -> p h t", p=128), kw.sem, reads=[KT[l].tr], writes=[kw.tr])
            for r in range(dil):
                nq_tot = T // dil
                for q0 in range(0, nq_tot, 128):
                    nq = min(128, nq_tot - q0)
                    sq0 = (tok0 // dil) + q0
                    n = sq0 // 128
                    i0 = sq0 % 128
                    qcol0 = r + dil * q0
                    for j in range(4):
                        hd = g * 4 + j
                        pn = S.ps()
                        pdn = S.ps()
                        kts = []
                        if n >= 1:
                            kts.append((0, (n - 1) * 128, 128))
                        kts.append((1, n * 128, i0 + nq))
                        for ki, (half, sk0, nk) in enumerate(kts):
                            tk0 = r + dil * sk0
                            kc0 = tk0 - w0
                            pst_ = S.ps()
                            op("pe", lambda e: e.matmul(pst_.t[0:nk, 0:nq], kw.t[:, j, kc0:kc0 + dil * (nk - 1) + 1:dil], q.t[:, hd, qcol0:qcol0 + dil * (nq - 1) + 1:dil], start=True, stop=True), reads=[kw.tr, q.tr], writes=[pst_.tr])
                            sc = ph.rot("sc", 3, [128, 128], F32)
                            op("dve", lambda e: e.scalar_tensor_tensor(out=sc.t[0:nk, 0:nq], in0=pst_.t[0:nk, 0:nq], scalar=SCALE, in1=tab.t[0:nk, (g * 4 + j) * 2 + half, i0:i0 + nq], op0=ALU.mult, op1=ALU.add), reads=[pst_.tr, tab.tr], writes=[sc.tr])
                            pt = ph.rot("pt", 3, [128, 128], BF16)
                            op("act", lambda e: e.activation(out=pt.t[0:nk, 0:nq], in_=sc.t[0:nk, 0:nq], func=AF.Exp), reads=[sc.tr], writes=[pt.tr])
                            vt = ph.rot("vt", 4, [128, 128], BF16)
                            dma("sp", vt.t[0:nk, :], VV[l].ap[tk0:tk0 + dil * (nk - 1) + 1:dil, hd * 128:(hd + 1) * 128], vt.sem, reads=[VV[l].tr], writes=[vt.tr])
                            op("pe", lambda e: e.matmul(pn.t[:, 0:nq], vt.t[0:nk, :], pt.t[0:nk, 0:nq], start=(ki == 0), stop=(ki == len(kts) - 1)), reads=[vt.tr, pt.tr], writes=[pn.tr])
                            op("pe", lambda e: e.matmul(pdn.t[:, 0:nq], onb.t[0:nk, :], pt.t[0:nk, 0:nq], start=(ki == 0), stop=(ki == len(kts) - 1)), reads=[onb.tr, pt.tr], writes=[pdn.tr])
                        ncols = slice(qcol0, qcol0 + dil * (nq - 1) + 1, dil)
                        if first:
                            op("act", lambda e: e.activation(out=num.t[:, j, ncols], in_=pn.t[:, 0:nq], func=AF.Copy), reads=[pn.tr], writes=[num.tr])
                            op("act", lambda e: e.activation(out=den.t[:, j, ncols], in_=pdn.t[:, 0:nq], func=AF.Copy), reads=[pdn.tr], writes=[den.tr])
                        else:
                            op("dve", lambda e: e.tensor_tensor(out=num.t[:, j, ncols], in0=num.t[:, j, ncols], in1=pn.t[:, 0:nq], op=ALU.add), reads=[pn.tr, num.tr], writes=[num.tr])
                            op("dve", lambda e: e.tensor_tensor(out=den.t[:, j, ncols], in0=den.t[:, j, ncols], in1=pdn.t[:, 0:nq], op=ALU.add), reads=[pdn.tr, den.tr], writes=[den.tr])
            first = False
        op("dve", lambda e: e.reciprocal(out=den.t[:], in_=den.t[:]), reads=[den.tr], writes=[den.tr])
        yb = ph.sb("yb", [128, 4, T], BF16, dma=True)
        op("dve", lambda e: e.tensor_tensor(out=yb.t[:], in0=num.t[:], in1=den.t[:], op=ALU.mult), reads=[num.tr, den.tr], writes=[yb.tr])
        dma("sp", YATT.ap[:, 0:T].rearrange("(h p) t -> p h t", p=128), yb.t[:], yb.sem, reads=[yb.tr], writes=[YATT.tr])
        ph.close()

    def ssd_sample(l):
        ph = Phase(S)
        T = NS
        pr = prm_sb[l]
        selh2 = ph.sb("selh2", [32, 16, 128], F32, dma=True)
        dma("sp", selh2.t[:], c_selh2.ap.rearrange("k (m p) -> k m p", m=16), selh2.sem, writes=[selh2.tr])
        xs = ph.sb("xs", [128, 16, T], F32, dma=True)
        dma("sp", xs.t[:], XBC.ap[0:2048, 0:T].rearrange("(kc p) t -> p kc t", p=128), xs.sem, reads=[XBC.tr], writes=[xs.tr])
        bcs = ph.sb("bcs", [128, 16, T], F32, dma=True)
        dma("sp", bcs.t[:], XBC.ap[2048:4096, 0:T].rearrange("(kc p) t -> p kc t", p=128), bcs.sem, reads=[XBC.tr], writes=[bcs.tr])
        dtag = ph.sb("dtag", [32, 2, T], F32, dma=True)
        dma("sp", dtag.t[:], DTAG.ap[0:64, 0:T].rearrange("(a h) t -> h a t", a=2), dtag.sem, reads=[DTAG.tr], writes=[dtag.tr])
        op("act", lambda e: e.activation(out=dtag.t[:, 1, :], in_=dtag.t[:, 1, :], func=AF.Exp), reads=[dtag.tr], writes=[dtag.tr])
        ex = ph.sb("ex", [128, 16, 2 * T], F32)
        for m in range(16):
            ps = S.ps()
            op("pe", lambda e: e.matmul(ps.t[:, 0:2 * T], selh2.t[:, m, :], dtag.t[:].rearrange("h a t -> h (a t)"), start=True, stop=True), reads=[selh2.tr, dtag.tr], writes=[ps.tr])
            op("act", lambda e: e.activation(out=ex.t[:, m, :], in_=ps.t[:, 0:2 * T], func=AF.Copy), reads=[ps.tr], writes=[ex.tr])
        xdt = ph.sb("xdt", [128, 16, T], F32)
        op("dve", lambda e: e.tensor_tensor(out=xdt.t[:], in0=xs.t[:], in1=ex.t[:, :, 0:T], op=ALU.mult), reads=[xs.tr, ex.tr], writes=[xdt.tr])
        ycol = ph.sb("ycol", [128, 16, T], F32, dma=True)
        op("dve", lambda e: e.memset(ycol.t[:], 0.0), writes=[ycol.tr])
        for s in range(NS):
            h0 = ph.rot("h0", 2, [128, 16, 128], F32)
            dma("sp", h0.t[:], st_h.ap[l, s].rearrange("(m p) n -> p m n", p=128), h0.sem, reads=[st_h.tr], writes=[h0.tr])
            hn = ph.rot("hn", 2, [128, 16, 128], F32)
            for g in range(8):
                bb = []
                for which in range(2):
                    dg = ph.rot("dgs", 3, [128, 128], F32)
                    op("dve", lambda e: e.tensor_scalar(out=dg.t[:], in0=ident.t[:], scalar1=bcs.t[:, which * 8 + g, s:s + 1], scalar2=None, op0=ALU.mult), reads=[ident.tr, bcs.tr], writes=[dg.tr])
                    pb = S.ps()
                    op("pe", lambda e: e.matmul(pb.t[:, 0:128], ones.t[:], dg.t[:], start=True, stop=True), reads=[ones.tr, dg.tr], writes=[pb.tr])
                    bb.append(pb)
                cbc = ph.rot("cbc", 2, [128, 128], F32)
                op("act", lambda e: e.activation(out=cbc.t[:], in_=bb[1].t[:, 0:128], func=AF.Copy), reads=[bb[1].tr], writes=[cbc.tr])
                for q in range(2):
                    m = g * 2 + q
                    bx = ph.rot("bx", 3, [128, 128], F32)
                    op("act", lambda e: e.activation(out=bx.t[:], in_=bb[0].t[:, 0:128], func=AF.Copy, scale=xdt.t[:, m, s:s + 1]), reads=[bb[0].tr, xdt.tr], writes=[bx.tr])
                    op("dve", lambda e: e.scalar_tensor_tensor(out=hn.t[:, m, :], in0=h0.t[:, m, :], scalar=ex.t[:, m, T + s:T + s + 1], in1=bx.t[:], op0=ALU.mult, op1=ALU.add), reads=[h0.tr, ex.tr, bx.tr], writes=[hn.tr])
                    junk = ph.rot("junk", 2, [128, 128], F32)
                    op("dve", lambda e: e.tensor_tensor(out=junk.t[:], in0=hn.t[:, m, :], in1=cbc.t[:], op=ALU.mult), reads=[hn.tr, cbc.tr], writes=[junk.tr])
                    op("dve", lambda e: e.tensor_reduce(out=ycol.t[:, m, s:s + 1], in_=junk.t[:], axis=AX.X, op=ALU.add), reads=[junk.tr], writes=[ycol.tr])
            dma("sp", o_sh.ap[l, s].rearrange("(m p) n -> p m n", p=128), hn.t[:], hn.sem, reads=[hn.tr], writes=[o_sh.tr])
        dma("sp", YT.ap[:, 0:T].rearrange("(kc p) t -> p kc t", p=128), ycol.t[:], ycol.sem, reads=[ycol.tr], writes=[YT.tr])
        ph.close()

    def att_sample(l):
        ph = Phase(S)
        T = NS
        sels = ph.sb("sels", [16, 16, 128], F32, dma=True)
        dma("sp", sels.t[:], c_sels.ap.rearrange("k (s p) -> k s p", s=16), sels.sem, writes=[sels.tr])
        tabs = ph.sb("tabs", [128, 12], F32, dma=True)
        dma("sp", tabs.t[:], c_tabs.ap, tabs.sem, writes=[tabs.tr])
        b0 = ph.sb("b0", [128, 12, T], F32, dma=True)
        dma("sp", b0.t[:], c_b0.ap.rearrange("p (h s) -> p h s", h=12), b0.sem, writes=[b0.tr])
        qtok = ph.sb("qtok", [16, 1536], F32, dma=True)
        dma("sp", qtok.t[:], QTOK.ap, qtok.sem, reads=[QTOK.tr], writes=[qtok.tr])
        qf = ph.sb("qf", [128, 12, T], F32, dma=True)
        dma("sp", qf.t[:], QF.ap.rearrange("(h p) t -> p h t", p=128), qf.sem, reads=[QF.tr], writes=[qf.tr])
        kf = ph.sb("kf", [128, 12, T], F32, dma=True)
        dma("sp", kf.t[:], KF.ap.rearrange("(h p) t -> p h t", p=128), kf.sem, reads=[KF.tr], writes=[kf.tr])
        vf = ph.sb("vf", [128, 12, T], F32, dma=True)
        dma("sp", vf.t[:], VF.ap.rearrange("(h p) t -> p h t", p=128), vf.sem, reads=[VF.tr], writes=[vf.tr])
        op("dve", lambda e: e.tensor_tensor(out=kf.t[:], in0=qf.t[:], in1=kf.t[:], op=ALU.mult), reads=[qf.tr, kf.tr], writes=[kf.tr])
        ps0 = S.ps()
        op("pe", lambda e: e.matmul(ps0.t[:, 0:12 * T], ones.t[:], kf.t[:].rearrange("p h t -> p (h t)"), start=True, stop=True), reads=[ones.tr, kf.tr], writes=[ps0.tr])
        p0 = ph.sb("p0", [128, 12, T], F32)
        op("dve", lambda e: e.scalar_tensor_tensor(out=p0.t[:].rearrange("p h t -> p (h t)"), in0=ps0.t[:, 0:12 * T], scalar=SCALE, in1=b0.t[:].rearrange("p h t -> p (h t)"), op0=ALU.mult, op1=ALU.add), reads=[ps0.tr, b0.tr], writes=[p0.tr])
        op("act", lambda e: e.activation(out=p0.t[:], in_=p0.t[:], func=AF.Exp), reads=[p0.tr], writes=[p0.tr])
        num = ph.sb("num", [128, 4, T], F32)
        den = ph.sb("den", [128, 4, T], F32)
        op("dve", lambda e: e.tensor_tensor(out=vf.t[:], in0=vf.t[:], in1=p0.t[:], op=ALU.mult), reads=[vf.tr, p0.tr], writes=[vf.tr])
        op("dve", lambda e: e.tensor_tensor(out=num.t[:], in0=vf.t[:, 0:4, :], in1=vf.t[:, 4:8, :], op=ALU.add), reads=[vf.tr], writes=[num.tr])
        op("dve", lambda e: e.tensor_tensor(out=num.t[:], in0=num.t[:], in1=vf.t[:, 8:12, :], op=ALU.add), reads=[vf.tr, num.tr], writes=[num.tr])
        op("dve", lambda e: e.tensor_tensor(out=den.t[:], in0=p0.t[:, 0:4, :], in1=p0.t[:, 4:8, :], op=ALU.add), reads=[p0.tr], writes=[den.tr])
        op("dve", lambda e: e.tensor_tensor(out=den.t[:], in0=den.t[:], in1=p0.t[:, 8:12, :], op=ALU.add), reads=[p0.tr, den.tr], writes=[den.tr])
        for g, (win, dil) in enumerate(GROUPS):
            pn = pslong
            for s in range(NS):
                kv = ph.rot("kv", 3, [128, 1024], F32)
                dma("sp", kv.t[:], caches[g].ap[l, s, 0:128 * dil:dil, :], kv.sem, reads=[caches[g].tr], writes=[kv.tr])
                pq = S.ps()
                op("pe", lambda e: e.matmul(pq.t[:], sels.t[:, s, :], qtok.t[:, g * 512:(g + 1) * 512], start=True, stop=True), reads=[sels.tr, qtok.tr], writes=[pq.tr])
                pr_ = ph.rot("pr", 2, [128, 512], F32)
                op("dve", lambda e: e.tensor_tensor(out=pr_.t[:], in0=kv.t[:, 0:512], in1=pq.t[:], op=ALU.mult), reads=[kv.tr, pq.tr], writes=[pr_.tr])
                sc = ph.rot("sc", 2, [128, 4], F32)
                op("dve", lambda e: e.tensor_reduce(out=sc.t[:], in_=pr_.t[:].rearrange("p (j d) -> p j d", j=4), axis=AX.X, op=ALU.add), reads=[pr_.tr], writes=[sc.tr])
                op("dve", lambda e: e.scalar_tensor_tensor(out=sc.t[:], in0=sc.t[:], scalar=SCALE, in1=tabs.t[:, g * 4:(g + 1) * 4], op0=ALU.mult, op1=ALU.add), reads=[sc.tr, tabs.tr], writes=[sc.tr])
                pp = ph.rot("pp", 2, [128, 4], F32)
                op("act", lambda e: e.activation(out=pp.t[:], in_=sc.t[:], func=AF.Exp), reads=[sc.tr], writes=[pp.tr])
                for j in range(4):
                    op("pe", lambda e: e.matmul(pn.t[:, s * 4 + j:s * 4 + j + 1], kv.t[:, 512 + j * 128:512 + (j + 1) * 128], pp.t[:, j:j + 1], start=True, stop=True, skip_group_check=True), reads=[kv.tr, pp.tr], writes=[pn.tr])
                op("pe", lambda e: e.matmul(pn.t[:, 64 + s * 4:64 + s * 4 + 4], ones.t[:], pp.t[:], start=True, stop=True, skip_group_check=True), reads=[ones.tr, pp.tr], writes=[pn.tr])
            op("dve", lambda e: e.tensor_tensor(out=num.t[:], in0=num.t[:], in1=pn.t[:, 0:64].rearrange("p (s j) -> p j s", j=4), op=ALU.add), reads=[num.tr, pn.tr], writes=[num.tr])
            op("dve", lambda e: e.tensor_tensor(out=den.t[:], in0=den.t[:], in1=pn.t[:, 64:128].rearrange("p (s j) -> p j s", j=4), op=ALU.add), reads=[den.tr, pn.tr], writes=[den.tr])
        op("dve", lambda e: e.reciprocal(out=den.t[:], in_=den.t[:]), reads=[den.tr], writes=[den.tr])
        yb = ph.sb("yb", [128, 4, T], BF16, dma=True)
        op("dve", lambda e: e.tensor_tensor(out=yb.t[:], in0=num.t[:], in1=den.t[:], op=ALU.mult), reads=[num.tr, den.tr], writes=[yb.tr])
        dma("sp", YATT.ap[:, 0:T].rearrange("(h p) t -> p h t", p=128), yb.t[:], yb.sem, reads=[yb.tr], writes=[YATT.tr])
        ph.close()

    def dense_layer_tail(l, T, xsrc, xc0, xdst, dc0):
        dense([(YSC, w_sc.ap[l], 16), (YSSM, w_ssm.ap[l], 16), (YATT, w_att.ap[l], 4)], [(0, D)], T, epi_merge, G=128)
        dense([(MRG, w_out.ap[l], 16)], [(0, D)], T, make_epi_res(xsrc, xc0, X1T, 0))
        norm_phase(X1T, 0, T, P_N2, l, HT)
        dense([(HT, w_up.ap[l], 16)], [(0, 4 * D)], T, epi_up)
        dense([(AT, w_dn.ap[l], 64)], [(0, D)], T, make_epi_res(X1T, 0, xdst, dc0), G=128)

    WIN_RANGES = DBG_RANGES or [(0, DT0), (DT0, 32), (QKV0, 4608)]

    try:
        for b in range(NB):
            in_transpose_src = DT_.__new__(DT_)
            in_transpose_src.ap = x_p.ap[b * TB:(b + 1) * TB, :]
            in_transpose_src.tr = x_p.tr
            in_transpose(in_transpose_src, TB, XT[0], b * TB)
        in_transpose(x_s, NS, XS[0], 0)
        if DBG_MODE == 'upA':
            dense([(HT, w_up.ap[0], 16)], [(0, 4 * D)], TB, epi_up)
            dense([(AT, w_dn.ap[0], 16)], [(0, D)], TB, make_epi_res(XT[0], 0, XT[1], 0))
            raise StopBuild()
        if DBG_MODE == 'upB':
            dense([(HT, w_up.ap[0], 16)], [(0, 2048)], TB, epi_up)
            dense([(AT, w_dn.ap[0], 64)], [(0, 512)], TB, make_epi_res(XT[0], 0, XT[1], 0), G=128)
            raise StopBuild()
        if DBG_MODE == 'updown':
            dense([(HT, w_up.ap[0], 16)], [(0, 4 * D)], TB, epi_up)
            dense([(AT, w_dn.ap[0], 64)], [(0, D)], TB, make_epi_res(XT[0], 0, XT[1], 0))
            raise StopBuild()
        if DBG_MODE == 'down2':
            dense([(AT, w_dn.ap[0], 64)], [(0, D)], TB, make_epi_res(XT[0], 0, XT[1], 0))
            dense([(AT, w_dn.ap[0], 64)], [(0, D)], TB, make_epi_res(XT[0], 0, XT[1], 0))
            raise StopBuild()
        if DBG_MODE == 'down':
            dense([(AT, w_dn.ap[0], 64)], [(0, DBG_N)], TB, make_epi_res(XT[0], 0, XT[1], 0))
            raise StopBuild()

        for l in range(DEPTH):
            for t_ in (halo_sc, halo_cv, Hst):
                op("dve", lambda e: e.memset(t_.t[:], 0.0), writes=[t_.tr])
            for b in range(NB):
                tok0 = b * TB
                norm_phase(XT[l], tok0, TB, P_N1, l, HT)
                for rg in WIN_RANGES:
                    dense([(HT, w_in.ap[l], 16)], [rg], TB, epi_win)
                conv_phase(l, TB, 16, 8192, halo_sc, P_SCW, 3, None, sc_store(l, TB, YSC))
                conv_phase(l, TB, 32, 14336, halo_cv, P_SSW, 4, P_SSB, cv_store(l, TB))
                dt_phase(l, TB, True)
                ssd_phase(l, TB)
                ssm_post(l, TB, YSSM)
                att_prep(l, TB, tok0, False)
                att_phase(l, TB, tok0)
                dense_layer_tail(l, TB, XT[l], tok0, XT[l + 1], tok0)
            halo_out(halo_sc, 16, 2, o_psc.ap[l], o_psc.tr)
            halo_out(halo_cv, 32, 3, o_pcv.ap[l], o_pcv.tr)
            hstate_out(o_ph.ap[l], o_ph.tr)
            if l == 0:
                hs_sc = S.mktile(es, "hs_sc", [128, 16, 2 * NS], F32, "sb", None)
                hs_cv = S.mktile(es, "hs_cv", [128, 32, 3 * NS], F32, "sb", None)
            halo_in(hs_sc, 16, 2, st_sc.ap[l], st_sc.tr, NS)
            halo_in(hs_cv, 32, 3, st_cv.ap[l], st_cv.tr, NS)
            norm_phase(XS[l], 0, NS, P_N1, l, HT)
            for rg in WIN_RANGES:
                dense([(HT, w_in.ap[l], 16)], [rg], NS, epi_win)
            sample_conv(S, nc, l, hs_sc, 16, 2, P_SCW, prm_sb, PROJ, XBC, YSC, o_ssc, True, ident)
            sample_conv(S, nc, l, hs_cv, 32, 3, P_SSW, prm_sb, PROJ, XBC, YSC, o_scv, False, ident)
            dt_phase(l, NS, False)
            ssd_sample(l)
            ssm_post(l, NS, YSSM)
            att_prep(l, NS, 0, True)
            att_sample(l)
            dense_layer_tail(l, NS, XS[l], 0, XS[l + 1], 0)

        for b in range(NB):
            dst = DT_.__new__(DT_)
            dst.ap = y_p.ap[b * TB:(b + 1) * TB, :]
            dst.tr = y_p.tr
            out_transpose(XT[DEPTH], b * TB, TB, dst)
        out_transpose(XS[DEPTH], 0, NS, y_s)
    except StopBuild:
        pass
    S.barrier()
    es.close()
    return nc


def sample_conv(S, nc, l, hs, nch, nh, wcol0, prm_sb, PROJ, XBC, YSC, o_st, is_sc, ident):
    op = S.op
    dma = S.dma
    T = NS
    ntap = nh + 1
    ph = Phase(S)
    pr = prm_sb[l]
    hv = hs.t[:].rearrange("p c (s j) -> p c s j", j=nh)
    newst = ph.sb("newst", [128, nch, T, nh], F32)
    for kc in range(nch):
        u = ph.rot("u", 3, [128, T], F32)
        if is_sc:
            cx = ph.rot("cx", 2, [128, 2, T], F32)
            dma("sp", cx.t[:], PROJ.ap[8192:12288, 0:T].rearrange("(b f) t -> f b t", b=2)[kc * 128:(kc + 1) * 128], cx.sem, reads=[PROJ.tr], writes=[cx.tr])
            op("dve", lambda e: e.tensor_tensor(out=u.t[:], in0=cx.t[:, 0, :], in1=cx.t[:, 1, :], op=ALU.mult), reads=[cx.tr], writes=[u.tr])
        else:
            dma("sp", u.t[:], PROJ.ap[14336 + kc * 128:14336 + (kc + 1) * 128, 0:T], u.sem, reads=[PROJ.tr], writes=[u.tr])
        cv = ph.rot("cv", 3, [128, T], F32)
        w0 = wcol0 + kc * ntap
        op("dve", lambda e: e.tensor_scalar(out=cv.t[:], in0=u.t[:], scalar1=pr.t[:, w0 + nh:w0 + nh + 1], scalar2=None, op0=ALU.mult), reads=[u.tr, pr.tr], writes=[cv.tr])
        for i in range(nh):
            op("dve", lambda e: e.scalar_tensor_tensor(out=cv.t[:], in0=hv[:, kc, :, i], scalar=pr.t[:, w0 + i:w0 + i + 1], in1=cv.t[:], op0=ALU.mult, op1=ALU.add), reads=[hs.tr, cv.tr, pr.tr], writes=[cv.tr])
        for i in range(nh - 1):
            op("act", lambda e: e.activation(out=newst.t[:, kc, :, i], in_=hv[:, kc, :, i + 1], func=AF.Copy), reads=[hs.tr], writes=[newst.tr])
        op("act", lambda e: e.activation(out=newst.t[:, kc, :, nh - 1], in_=u.t[:], func=AF.Copy), reads=[u.tr], writes=[newst.tr])
        if is_sc:
            b = ph.rot("b", 2, [128, T], F32)
            dma("sp", b.t[:], PROJ.ap[6144 + kc * 128:6144 + (kc + 1) * 128, 0:T], b.sem, reads=[PROJ.tr], writes=[b.tr])
            st = ph.rot("st", 3, [128, T], BF16)
            op("dve", lambda e: e.tensor_tensor(out=st.t[:], in0=cv.t[:], in1=b.t[:], op=ALU.mult), reads=[cv.tr, b.tr], writes=[st.tr])
            dma("sp", YSC.ap[kc * 128:(kc + 1) * 128, 0:T], st.t[:], st.sem, reads=[st.tr], writes=[YSC.tr])
        else:
            st = ph.rot("st", 3, [128, T], F32)
            op("act", lambda e: e.activation(out=st.t[:], in_=cv.t[:], func=AF.Silu, bias=pr.t[:, 240 + kc:240 + kc + 1]), reads=[cv.tr, pr.tr], writes=[st.tr])
            dma("sp", XBC.ap[kc * 128:(kc + 1) * 128, 0:T], st.t[:], st.sem, reads=[st.tr], writes=[XBC.tr])
    rows = T * nh
    o = ph.sb("o", [rows, nch * 128], F32, dma=True)
    for kc in range(nch):
        ps = S.ps()
        op("pe", lambda e: e.transpose(ps.t[0:rows, 0:128], newst.t[:, kc, :, :].rearrange("p s j -> p (s j)"), ident.t[:]), reads=[newst.tr, ident.tr], writes=[ps.tr])
        op("dve", lambda e: e.tensor_copy(out=o.t[0:rows, kc * 128:(kc + 1) * 128], in_=ps.t[0:rows, 0:128]), reads=[ps.tr], writes=[o.tr])
    dma("sp", o_st.ap[l], o.t[:], o.sem, reads=[o.tr], writes=[o_st.tr])
    ph.close()


def _t5_bucket(d):
    d = np.asarray(d, np.int64)
    max_exact = 16
    df = np.maximum(d, 1).astype(np.float32)
    large = max_exact + (np.log(df / max_exact) / math.log(2048 / max_exact) * (32 - max_exact)).astype(np.int32)
    large = np.minimum(large, 31)
    return np.where(d < max_exact, d, large)


def _consts(rel_bias):
    c = {}
    c["c_ident"] = np.eye(128, dtype=np.float32)
    c["c_ones"] = np.ones((128, 128), np.float32)
    j = np.arange(128)[:, None]
    i = np.arange(128)[None, :]
    m = np.where(i < j, NEG, 0.0).astype(np.float32)
    c["c_mask4"] = np.tile(m, (1, 4))
    selh = np.zeros((32, 32, 128), np.float32)
    for h in range(32):
        selh[h, h, :] = 1.0
    c["c_selh"] = selh.reshape(32, -1)
    selh2 = np.zeros((32, 16, 128), np.float32)
    for mm in range(16):
        selh2[2 * mm, mm, 0:64] = 1.0
        selh2[2 * mm + 1, mm, 64:128] = 1.0
    c["c_selh2"] = selh2.reshape(32, -1)
    sels = np.zeros((16, 16, 128), np.float32)
    for s in range(16):
        sels[s, s, :] = 1.0
    c["c_sels"] = sels.reshape(16, -1)
    rb = np.concatenate([np.asarray(rel_bias, np.float32), np.full((1, 12), NEG, np.float32)], axis=0)
    kj = np.arange(256)[:, None]
    qi = np.arange(128)[None, :]
    sdist = qi + 128 - kj
    band = (sdist >= 0) & (sdist <= 128)
    tab = np.zeros((128, 3, 4, 2, 128), np.float32)
    tabs = np.zeros((128, 3, 4), np.float32)
    b0 = np.zeros((128, 12, NS), np.float32)
    for g, (w, dil) in enumerate(GROUPS):
        bidx = _t5_bucket(np.clip(sdist, 0, 128) * dil)
        bidx = np.where(band, bidx, 32)
        steps = 128 - np.arange(128)
        sidx = _t5_bucket(steps * dil)
        for jh in range(4):
            t = rb[bidx, g * 4 + jh]
            tab[:, g, jh, 0, :] = t[0:128]
            tab[:, g, jh, 1, :] = t[128:256]
            tabs[:, g, jh] = rb[sidx, g * 4 + jh]
            b0[:, g * 4 + jh, :] = rb[0, g * 4 + jh]
    c["c_tab"] = tab.reshape(128, -1)
    c["c_tabs"] = tabs.reshape(128, -1)
    c["c_b0"] = b0.reshape(128, -1)
    return c


def _pack_prm(inp, l):
    p = np.zeros((128, NPRM), np.float32)

    def col(v, n):
        return np.asarray(v, np.float32).reshape(n, 128).T

    p[:, 0:16] = col(inp["norm1_g"][l], 16)
    p[:, 16:32] = col(inp["norm2_g"][l], 16)
    p[:, 32:48] = col(inp["ssm_norm_g"][l], 16)
    p[:, 48:64] = col(np.repeat(np.asarray(inp["ssm_D"][l], np.float32), 64), 16)
    scw = np.asarray(inp["sc_conv_w"][l], np.float32)
    p[:, 64:112] = scw.reshape(3, 16, 128).transpose(2, 1, 0).reshape(128, 48)
    ssw = np.asarray(inp["ssm_conv_w"][l], np.float32)
    p[:, 112:240] = ssw.reshape(4, 32, 128).transpose(2, 1, 0).reshape(128, 128)
    p[:, 240:272] = col(inp["ssm_conv_b"][l], 32)
    p[:, 272] = np.asarray(inp["q_norm_g"][l], np.float32)
    p[:, 273] = np.asarray(inp["k_norm_g"][l], np.float32)
    p[0:32, 274] = np.asarray(inp["ssm_dt_bias"][l], np.float32)
    p[0:32, 275] = np.asarray(inp["ssm_A_log"][l], np.float32)
    return p


_NC_CACHE = {}


def kernel(**inp):
    x_prompt = np.asarray(inp["x_prompt"], np.float32)
    B, SEQ, _ = x_prompt.shape
    x_sample = np.asarray(inp["x_sample"], np.float32)
    DB = x_sample.shape[0]
    ncore = 8
    assert DB == ncore * NS
    if SEQ not in _NC_CACHE:
        _NC_CACHE[SEQ] = build(SEQ)
    nc = _NC_CACHE[SEQ]
    consts = _consts(inp["rel_bias"])
    prm = np.stack([_pack_prm(inp, l) for l in range(DEPTH)])
    shared = dict(consts)
    shared["prm"] = prm
    shared["w_in"] = np.asarray(inp["w_in"], np.float32)
    shared["w_sc"] = np.asarray(inp["w_br_sc"], np.float32)
    shared["w_ssm"] = np.asarray(inp["w_br_ssm"], np.float32)
    shared["w_att"] = np.asarray(inp["w_br_att"], np.float32)
    shared["w_out"] = np.asarray(inp["w_out"], np.float32)
    shared["w_up"] = np.asarray(inp["w_up"], np.float32)
    shared["w_dn"] = np.asarray(inp["w_down"], np.float32)
    ssc = np.asarray(inp["state_sc_conv"], np.float32)
    scv = np.asarray(inp["state_ssm_conv"], np.float32)
    sh = np.asarray(inp["state_ssm"], np.float32)
    cas = [np.asarray(inp[k], np.float32) for k in ("cache_kv_w128", "cache_kv_w512", "cache_kv_w2048")]
    in_maps = []
    for c in range(ncore):
        sl = slice(c * NS, (c + 1) * NS)
        m = dict(shared)
        m["x_p"] = np.ascontiguousarray(x_prompt[c % B])
        m["x_s"] = np.ascontiguousarray(x_sample[sl, 0, :])
        m["st_sc"] = np.ascontiguousarray(ssc[:, sl].reshape(DEPTH, NS * 2, D))
        m["st_cv"] = np.ascontiguousarray(scv[:, sl].reshape(DEPTH, NS * 3, 4096))
        m["st_h"] = np.ascontiguousarray(sh[:, sl].reshape(DEPTH, NS, 2048, 128))
        for g in range(3):
            a = cas[g][:, sl]
            m["ca%d" % g] = np.ascontiguousarray(a.reshape(DEPTH, NS, a.shape[2], 1024))
        in_maps.append(m)
    res = run_bass_kernel_spmd(nc, in_maps, core_ids=list(range(ncore))).results
    KEEP = [min(w, SEQ) for w, dl in GROUPS]
    y_prompt = np.stack([res[b]["y_p"] for b in range(B)])
    y_sample = np.concatenate([res[c]["y_s"] for c in range(ncore)])[:, None, :]
    p_sc = np.stack([res[b]["o_psc"] for b in range(B)], axis=1)
    p_cv = np.stack([res[b]["o_pcv"] for b in range(B)], axis=1)
    p_h = np.stack([res[b]["o_ph"].reshape(DEPTH, 32, 64, 128) for b in range(B)], axis=1)
    p_kv = [np.stack([res[b]["o_pkv%d" % g].reshape(DEPTH, KEEP[g], 2, 4, 128) for b in range(B)], axis=1) for g in range(3)]
    s_sc = np.concatenate([res[c]["o_ssc"].reshape(DEPTH, NS, 2, D) for c in range(ncore)], axis=1)
    s_cv = np.concatenate([res[c]["o_scv"].reshape(DEPTH, NS, 3, 4096) for c in range(ncore)], axis=1)
    s_h = np.concatenate([res[c]["o_sh"].reshape(DEPTH, NS, 32, 64, 128) for c in range(ncore)], axis=1)
    s_kv = [np.concatenate([res[c]["o_skv%d" % g].reshape(DEPTH, NS, 1, 2, 4, 128) for c in range(ncore)], axis=1) for g in range(3)]
    outs = (y_prompt, y_sample, p_sc, p_cv, p_h, p_kv[0], p_kv[1], p_kv[2], s_sc, s_cv, s_h, s_kv[0], s_kv[1], s_kv[2])
    return tuple(np.ascontiguousarray(o, dtype=np.float32) for o in outs)
```
